# Optimizing a Trainium2 kernel written in Bass

```python
import math
import jax, jax.numpy as jnp
from jax import lax
import numpy as np

D_MODEL = 1024
BATCH = 4
SEQ = 4096
DEPTH = 2

N_META = 16
BLOCK = 128
PAD = BLOCK - N_META
ROPE_THETA = 10000.0
NORM_EPS = 1e-6
MASK_VALUE = -1e30

MLA_HEADS = 8
MLA_Q_LORA = 256
MLA_KV_LORA = 128
MLA_NOPE = 64
MLA_ROPE = 32
MLA_V = 64
MLA_QK = MLA_NOPE + MLA_ROPE
SB_HEADS = 8
SB_HEAD_DIM = 64
DIFF_HEADS = 4
DIFF_HEAD_DIM = 64
DIFF_V_DIM = 2 * DIFF_HEAD_DIM
N_BRANCHES = 3
MLA_OUT = MLA_HEADS * MLA_V
SB_OUT = SB_HEADS * SB_HEAD_DIM
DIFF_QK = DIFF_HEADS * 2 * DIFF_HEAD_DIM
DIFF_OUT = DIFF_HEADS * DIFF_V_DIM
BRANCH_WIDTH = MLA_OUT + SB_OUT + DIFF_OUT
D_FF = 4 * D_MODEL
IN_SIZES = (MLA_Q_LORA, MLA_KV_LORA, MLA_ROPE,
            SB_OUT, SB_OUT, SB_OUT,
            DIFF_QK, DIFF_QK, DIFF_OUT,
            N_BRANCHES * D_MODEL)
IN_COLS = sum(IN_SIZES)

kernel_name = "hybrid_mla_stickbreak_diffattn_gated"


def _split_cols(t, sizes):
    outs, off = [], 0
    for s in sizes:
        outs.append(t[..., off:off + s])
        off += s
    return outs


def _rmsnorm(x, g):
    x32 = x.astype(jnp.float32)
    y = x32 * lax.rsqrt(jnp.mean(x32 * x32, axis=-1, keepdims=True) + NORM_EPS)
    return (y * g.astype(jnp.float32)).astype(x.dtype)


def _rope(x, pos):
    d = x.shape[-1]
    half = d // 2
    inv_freq = jnp.exp(-math.log(ROPE_THETA) * (2.0 * jnp.arange(half, dtype=jnp.float32) / d))
    ang = pos.astype(jnp.float32)[:, None] * inv_freq[None, :]
    cos = jnp.cos(ang).astype(x.dtype)
    sin = jnp.sin(ang).astype(x.dtype)
    x1, x2 = x[..., :half], x[..., half:]
    return jnp.concatenate([x1 * cos - x2 * sin, x1 * sin + x2 * cos], axis=-1)


def _heads(t, n_heads):
    b, p, _ = t.shape
    return t.reshape(b, p, n_heads, -1).transpose(0, 2, 1, 3)


def _to_blocks(t):
    b, h, p, d = t.shape
    return t.reshape(b, h, p // BLOCK, BLOCK, d).transpose(2, 0, 1, 3, 4)


def _from_blocks(o):
    nb, b, h, blk, d = o.shape
    return o.transpose(1, 0, 3, 2, 4).reshape(b, nb * blk, h * d)


def _scores(qb, k, scale):
    return jnp.einsum('bhqd,bhkd->bhqk', qb, k).astype(jnp.float32) * scale


def _masked_softmax(s, mask):
    return jax.nn.softmax(jnp.where(mask[None, None], s, MASK_VALUE), axis=-1)


def _softmax_attention(q, k, v, scale):
    p_len = q.shape[2]
    kidx = jnp.arange(p_len)
    kvalid = kidx >= PAD

    def block(args):
        qb, qi = args
        mask = (kidx[None, :] <= qi[:, None]) & kvalid[None, :]
        p = _masked_softmax(_scores(qb, k, scale), mask)
        return jnp.einsum('bhqk,bhkd->bhqd', p.astype(v.dtype), v)

    return _from_blocks(lax.map(block, (_to_blocks(q), kidx.reshape(-1, BLOCK))))


def _stick_breaking_attention(q, k, v):
    p_len = q.shape[2]
    scale = q.shape[-1] ** -0.5
    kidx = jnp.arange(p_len)
    kvalid = kidx >= PAD

    def block(args):
        qb, qi = args
        z = _scores(qb, k, scale)
        mask = ((kidx[None, :] < qi[:, None]) & kvalid[None, :])[None, None]
        log_keep = jnp.where(mask, jax.nn.log_sigmoid(-z), 0.0)
        suffix = lax.cumsum(log_keep, axis=3, reverse=True) - log_keep
        a = jnp.where(mask, jnp.exp(jax.nn.log_sigmoid(z) + suffix), 0.0)
        return jnp.einsum('bhqk,bhkd->bhqd', a.astype(v.dtype), v)

    return _from_blocks(lax.map(block, (_to_blocks(q), kidx.reshape(-1, BLOCK))))


def _differential_attention(q1, q2, k1, k2, v, lam):
    p_len = q1.shape[2]
    scale = q1.shape[-1] ** -0.5
    kidx = jnp.arange(p_len)
    kvalid = kidx >= PAD

    def block(args):
        q1b, q2b, qi = args
        mask = (kidx[None, :] <= qi[:, None]) & kvalid[None, :]
        p1 = _masked_softmax(_scores(q1b, k1, scale), mask)
        p2 = _masked_softmax(_scores(q2b, k2, scale), mask)
        w = p1 - lam * p2
        return jnp.einsum('bhqk,bhkd->bhqd', w.astype(v.dtype), v)

    return _from_blocks(lax.map(block, (_to_blocks(q1), _to_blocks(q2), kidx.reshape(-1, BLOCK))))


def setup_inputs(seed: int = 0) -> dict:
    key = jax.random.key(seed)
    ks = jax.random.split(key, 24)
    f32 = jnp.float32

    def nrm(k, shape, scale):
        return jax.random.normal(k, shape, f32) * scale

    def gain(k, shape):
        return 1.0 + 0.02 * jax.random.normal(k, shape, f32)

    return {
        "x": jax.random.normal(ks[0], (BATCH, SEQ, D_MODEL), f32),
        "meta_tokens": nrm(ks[1], (N_META, D_MODEL), 1.0),
        "ln1_g": gain(ks[2], (DEPTH, D_MODEL)),
        "w_in": nrm(ks[3], (DEPTH, D_MODEL, IN_COLS), D_MODEL ** -0.5),
        "mla_cq_norm_g": gain(ks[4], (DEPTH, MLA_Q_LORA)),
        "mla_ckv_norm_g": gain(ks[5], (DEPTH, MLA_KV_LORA)),
        "mla_w_uq": nrm(ks[6], (DEPTH, MLA_Q_LORA, MLA_HEADS * MLA_QK), MLA_Q_LORA ** -0.5),
        "mla_w_ukv": nrm(ks[7], (DEPTH, MLA_KV_LORA, MLA_HEADS * (MLA_NOPE + MLA_V)), MLA_KV_LORA ** -0.5),
        "mla_q_norm_g": gain(ks[8], (DEPTH, MLA_QK)),
        "mla_k_norm_g": gain(ks[9], (DEPTH, MLA_QK)),
        "diff_q_norm_g": gain(ks[10], (DEPTH, DIFF_HEAD_DIM)),
        "diff_k_norm_g": gain(ks[11], (DEPTH, DIFF_HEAD_DIM)),
        "diff_lambda": nrm(ks[12], (DEPTH, 4, DIFF_HEAD_DIM), 0.1),
        "diff_out_norm_g": gain(ks[13], (DEPTH, DIFF_V_DIM)),
        "gate_b": nrm(ks[14], (DEPTH, N_BRANCHES * D_MODEL), 0.02),
        "w_branch": nrm(ks[15], (DEPTH, BRANCH_WIDTH, D_MODEL), MLA_OUT ** -0.5),
        "w_out": nrm(ks[16], (DEPTH, D_MODEL, D_MODEL), D_MODEL ** -0.5),
        "ln2_g": gain(ks[17], (DEPTH, D_MODEL)),
        "w_ff1": nrm(ks[18], (DEPTH, D_MODEL, D_FF), D_MODEL ** -0.5),
        "w_ff2": nrm(ks[19], (DEPTH, D_FF, D_MODEL), D_FF ** -0.5),
    }


def reference(x, meta_tokens, ln1_g, w_in, mla_cq_norm_g, mla_ckv_norm_g, mla_w_uq, mla_w_ukv,
              mla_q_norm_g, mla_k_norm_g, diff_q_norm_g, diff_k_norm_g, diff_lambda,
              diff_out_norm_g, gate_b, w_branch, w_out, ln2_g, w_ff1, w_ff2):
    b, seq, d = x.shape
    pad = jnp.zeros((b, PAD, d), x.dtype)
    meta = jnp.broadcast_to(meta_tokens.astype(x.dtype)[None], (b, N_META, d))
    h_res = jnp.concatenate([pad, meta, x], axis=1)
    p_len = h_res.shape[1]
    pos = jnp.maximum(jnp.arange(p_len) - PAD, 0)

    for layer in range(DEPTH):
        h = _rmsnorm(h_res, ln1_g[layer])
        proj = h @ w_in[layer]
        (c_q, c_kv, k_rope, sb_q, sb_k, sb_v,
         df_q, df_k, df_v, gate_logits) = _split_cols(proj, IN_SIZES)

        q = _heads(_rmsnorm(c_q, mla_cq_norm_g[layer]) @ mla_w_uq[layer], MLA_HEADS)
        kv = _heads(_rmsnorm(c_kv, mla_ckv_norm_g[layer]) @ mla_w_ukv[layer], MLA_HEADS)
        k_nope, v_mla = kv[..., :MLA_NOPE], kv[..., MLA_NOPE:]
        k_r = jnp.broadcast_to(k_rope[:, None], (b, MLA_HEADS, p_len, MLA_ROPE))
        k = jnp.concatenate([k_nope, k_r], axis=-1)
        q = _rmsnorm(q, mla_q_norm_g[layer])
        k = _rmsnorm(k, mla_k_norm_g[layer])
        q = jnp.concatenate([q[..., :MLA_NOPE], _rope(q[..., MLA_NOPE:], pos)], axis=-1)
        k = jnp.concatenate([k[..., :MLA_NOPE], _rope(k[..., MLA_NOPE:], pos)], axis=-1)
        out_a = _softmax_attention(q, k, v_mla, MLA_QK ** -0.5)

        out_b = _stick_breaking_attention(_heads(sb_q, SB_HEADS), _heads(sb_k, SB_HEADS),
                                          _heads(sb_v, SB_HEADS))

        dq = _heads(df_q, DIFF_HEADS)
        dk = _heads(df_k, DIFF_HEADS)
        dv = _heads(df_v, DIFF_HEADS)
        qn, kn = diff_q_norm_g[layer], diff_k_norm_g[layer]
        q1 = _rope(_rmsnorm(dq[..., :DIFF_HEAD_DIM], qn), pos)
        q2 = _rope(_rmsnorm(dq[..., DIFF_HEAD_DIM:], qn), pos)
        k1 = _rope(_rmsnorm(dk[..., :DIFF_HEAD_DIM], kn), pos)
        k2 = _rope(_rmsnorm(dk[..., DIFF_HEAD_DIM:], kn), pos)
        lam_init = 0.8 - 0.6 * math.exp(-0.3 * layer)
        lp = diff_lambda[layer].astype(jnp.float32)
        lam = jnp.exp(jnp.sum(lp[0] * lp[1])) - jnp.exp(jnp.sum(lp[2] * lp[3])) + lam_init
        o_c = _differential_attention(q1, q2, k1, k2, dv, lam)
        o_c = _rmsnorm(o_c.reshape(b, p_len, DIFF_HEADS, DIFF_V_DIM), diff_out_norm_g[layer])
        out_c = (o_c * (1.0 - lam_init)).reshape(b, p_len, DIFF_OUT)

        gates = jax.nn.sigmoid(gate_logits + gate_b[layer]).reshape(b, p_len, N_BRANCHES, d)
        wb_a, wb_b, wb_c = _split_cols(w_branch[layer].T, (MLA_OUT, SB_OUT, DIFF_OUT))
        merged = (gates[:, :, 0] * (out_a @ wb_a.T)
                  + gates[:, :, 1] * (out_b @ wb_b.T)
                  + gates[:, :, 2] * (out_c @ wb_c.T))
        h_res = h_res + merged @ w_out[layer]

        h2 = _rmsnorm(h_res, ln2_g[layer])
        h_res = h_res + jnp.square(jax.nn.relu(h2 @ w_ff1[layer])) @ w_ff2[layer]

    return h_res[:, PAD + N_META:]
```

```python
import math
from contextlib import ExitStack

import numpy as np
import ml_dtypes

import concourse.bass as bass
import concourse.mybir as mybir
from concourse.bass_utils import run_bass_kernel_spmd

F32 = mybir.dt.float32
BF16 = mybir.dt.bfloat16
AF = mybir.ActivationFunctionType
ALU = mybir.AluOpType

D = 1024
SEQ = 4096
NB = 33
PTOK = NB * 128
PAD = 112
EPS = 1e-6
IN_COLS = 6560
DFF = 4096
C_CQ, C_CKV, C_KR, C_SBQ, C_SBK, C_SBV, C_DFQ, C_DFK, C_DFV, C_GATE = (
    0, 256, 384, 416, 928, 1440, 1952, 2464, 2976, 3488)
TILES = [(0, 1)] + [(1 + 4 * i, 4) for i in range(8)]

SP_LN1, SP_LN2, SP_CQ, SP_CKV, SP_MQ, SP_MK, SP_DQ, SP_DK, SP_GB = 0, 8, 16, 18, 19, 20, 21, 22, 23
SP_N = 47
CM_NEGTRI, CM_MINCL, CM_MSTRICT, CM_RD, CM_RM, CM_SEL, CM_NEGCOL, CM_CARRYL, CM_N = 0, 1, 2, 3, 4, 5, 6, 7, 8
CF_IDENT, CF_ONES, CF_BONES, CF_N = 0, 1, 2, 3


class Unit:
    __slots__ = ("w", "ws", "rs", "name", "excl")

    def __init__(self, name=""):
        self.excl = False
        self.w = None
        self.ws = []
        self.rs = []
        self.name = name


class View:
    __slots__ = ("ap", "u")

    def __init__(self, ap, u=None, name=""):
        self.ap = ap
        self.u = u if u is not None else Unit(name)


class Ring:
    def __init__(self, items):
        self.items = list(items)
        self.i = 0

    def next(self):
        v = self.items[self.i]
        self.i = (self.i + 1) % len(self.items)
        return v


class Eng:
    def __init__(self, name):
        self.name = name
        self.q = []
        self.n = 0
        self.seen = {}
        self.sem = None
        self.key = None


NDS = 12


class KB:
    def __init__(self, nc, es):
        self.nc = nc
        self.sems = {}
        self.engs = {}
        for n in ("pe", "act", "dve", "pool", "sp"):
            e = Eng(n)
            e.sem = es.enter_context(nc.semaphore("s_" + n))
            e.key = "s_" + n
            self.sems[e.key] = e.sem
            self.engs[n] = e
        self.dring = {}
        for q in ("sp", "pool"):
            keys = []
            for i in range(NDS):
                k = f"d_{q}{i}"
                self.sems[k] = es.enter_context(nc.semaphore(k))
                keys.append(k)
            self.dring[q] = {"keys": keys, "cnt": [0] * NDS, "next": 0}
        self.nwaits = 0
        self.nops = 0
        self.max_ops = None
        self.log = []

    def _collect(self, eng, reads, writes):
        toks = {}

        def add(t, war=False):
            if t is None:
                return
            k, v = t
            if k == eng.key:
                if eng.name == "pe" or (war and eng.name != "pool"):
                    return
            if toks.get(k, 0) < v:
                toks[k] = v
        for u in reads:
            add(u.w)
            for t in u.ws:
                add(t)
            if u.excl:
                for t in u.rs:
                    add(t, war=True)
        for u in writes:
            add(u.w)
            for t in u.ws:
                add(t)
            for t in u.rs:
                add(t, war=True)
        waits = []
        for k, v in toks.items():
            if eng.seen.get(k, 0) >= v:
                continue
            eng.seen[k] = v
            waits.append((k, v))
        return waits

    @staticmethod
    def _units(lst):
        out = []
        for x in lst:
            if isinstance(x, Unit):
                out.append(x)
            elif isinstance(x, View):
                out.append(x.u)
            else:
                out.extend(KB._units(x))
        return out

    def op(self, en, fn, reads=(), writes=(), sig=True):
        if self.max_ops is not None and self.nops >= self.max_ops:
            return
        eng = self.engs[en]
        if self.max_ops is not None:
            import sys as _s
            f = _s._getframe(1)
            while f.f_code.co_name in ("mm", "tr", "act", "tt", "ts", "stt", "copy", "memset", "recip", "op"):
                f = f.f_back
            self.log.append((self.nops, en, f.f_lineno, f.f_code.co_name))
        reads = self._units(reads)
        writes = self._units(writes)
        waits = self._collect(eng, reads, writes)
        tok = (eng.key, eng.n + 1)
        if sig:
            eng.n += 1
        eng.q.append((waits, fn, "inc" if sig else None, None))
        for u in reads:
            u.rs.append(tok)
        for u in writes:
            u.w = tok
            u.ws = []
            u.rs = []
        self.nops += 1
        self.nwaits += len(waits)

    def dma(self, q, out, in_, reads=(), writes=(), acc=(), **kw):
        if self.max_ops is not None and self.nops >= self.max_ops:
            return
        eng = self.engs[q]
        if self.max_ops is not None:
            import sys as _s
            f = _s._getframe(1)
            self.log.append((self.nops, "dma_" + q, f.f_lineno, f.f_code.co_name))
        ring = self.dring[q]
        i = ring["next"]
        ring["next"] = (i + 1) % NDS
        reads = self._units(reads)
        writes = self._units(writes)
        waits = self._collect(eng, reads, writes)
        k = ring["keys"][i]
        prev = ring["cnt"][i]
        if prev > 0 and eng.seen.get(k, 0) < prev:
            eng.seen[k] = prev
            waits.append((k, prev))
        ring["cnt"][i] = prev + 16
        tok = (k, prev + 16)
        eng.q.append((waits, (lambda e: e.dma_start(out=out, in_=in_, **kw)), "dma", k))
        for u in reads:
            u.rs.append(tok)
        for u in writes:
            u.w = tok
            u.ws = []
            u.rs = []
        for u in self._units(acc):
            u.ws.append(tok)
        self.nops += 1
        self.nwaits += len(waits)

    def final_wait(self, en, units):
        eng = self.engs[en]
        waits = self._collect(eng, self._units(units), [])
        eng.q.append((waits, None, None, None))

    def emit(self, block):
        def mk(eng):
            def f(e):
                for waits, fn, kind, k in eng.q:
                    for wk, wv in waits:
                        e.wait_ge(self.sems[wk], wv)
                    if fn is None:
                        continue
                    ins = fn(e)
                    if kind == "inc":
                        ins.then_inc(eng.sem, 1)
                    elif kind == "dma":
                        ins.then_inc(self.sems[k], 16)
            return f
        block.tensor(mk(self.engs["pe"]))
        block.scalar(mk(self.engs["act"]))
        block.vector(mk(self.engs["dve"]))
        block.gpsimd(mk(self.engs["pool"]))
        block.sync(mk(self.engs["sp"]))

    def mm(self, out, lhsT, rhs, start, stop, reads, writes, sig=True):
        self.op("pe", lambda e: e.matmul(out, lhsT=lhsT, rhs=rhs, start=start, stop=stop,
                                         skip_group_check=True), reads, writes, sig)

    def tr(self, out, in_, ident, reads, writes):
        self.op("pe", lambda e: e.matmul(out, lhsT=in_, rhs=ident, start=True, stop=True,
                                         skip_group_check=True), reads, writes)

    def act(self, out, in_, func, reads, writes, bias=None, scale=None, accum_out=None):
        kw = {}
        if bias is not None:
            kw["bias"] = bias
        if scale is not None:
            kw["scale"] = scale
        if accum_out is not None:
            kw["accum_out"] = accum_out
        self.op("act", lambda e: e.activation(out=out, in_=in_, func=func, **kw), reads, writes)

    def tt(self, en, out, in0, in1, op, reads, writes):
        self.op(en, lambda e: e.tensor_tensor(out=out, in0=in0, in1=in1, op=op), reads, writes)

    def ts(self, en, out, in0, s1, s2, op0, op1, reads, writes):
        if op1 is None:
            self.op(en, lambda e: e.tensor_scalar(out=out, in0=in0, scalar1=s1, scalar2=None, op0=op0),
                    reads, writes)
        else:
            self.op(en, lambda e: e.tensor_scalar(out=out, in0=in0, scalar1=s1, scalar2=s2, op0=op0, op1=op1),
                    reads, writes)

    def stt(self, out, in0, scalar, in1, op0, op1, reads, writes):
        self.op("dve", lambda e: e.scalar_tensor_tensor(out=out, in0=in0, scalar=scalar, in1=in1,
                                                        op0=op0, op1=op1), reads, writes)

    def copy(self, en, out, in_, reads, writes):
        if en == "act":
            self.op("act", lambda e: e.activation(out=out, in_=in_, func=AF.Copy), reads, writes)
        else:
            self.op(en, lambda e: e.tensor_copy(out=out, in_=in_), reads, writes)

    def memset(self, en, ap, val, writes):
        self.op(en, lambda e: e.memset(ap, val), (), writes)

    def recip(self, out, in_, reads, writes):
        self.op("dve", lambda e: e.reciprocal(out=out, in_=in_), reads, writes)


def _consts():
    cm = np.zeros((CM_N, 128, 128), np.float32)
    kp = np.arange(128)[:, None]
    kk = np.arange(128)[None, :]
    cm[CM_NEGTRI] = -(kp >= kk).astype(np.float32)
    cm[CM_MINCL] = (kp <= kk).astype(np.float32)
    cm[CM_MSTRICT] = (kp < kk).astype(np.float32)
    for d in range(128):
        g = (d // 64) * 64
        cm[CM_RD, g + ((d - g + 32) % 64), d] = 1.0
    for d in range(64, 96):
        cm[CM_RM, 64 + ((d - 64 + 16) % 32), d] = 1.0
    for r in range(32):
        cm[CM_SEL, r, 64 + r] = 1.0
    cm[CM_NEGCOL][:, 0] = -1.0
    cm[CM_NEGCOL][:, 64] = -1.0
    cm[CM_CARRYL][0, :] = 1.0
    cm[CM_CARRYL][64, :] = 1.0
    cf = np.zeros((CF_N, 128, 128), np.float32)
    cf[CF_IDENT] = np.eye(128, dtype=np.float32)
    cf[CF_ONES] = 1.0
    cf[CF_BONES][:64, :64] = 1.0
    cf[CF_BONES][64:, 64:] = 1.0
    pos = np.maximum(np.arange(PTOK) - PAD, 0).astype(np.float32)
    inv = np.exp(-math.log(10000.0) * (2.0 * np.arange(32, dtype=np.float32) / 64)).astype(np.float32)
    ang = pos[None, :] * inv[:, None]
    c64 = np.concatenate([np.cos(ang), np.cos(ang)], 0)
    s64 = np.concatenate([-np.sin(ang), np.sin(ang)], 0)
    ropd = np.stack([np.concatenate([c64, c64], 0), np.concatenate([s64, s64], 0)], 0).astype(np.float32)
    inv = np.exp(-math.log(10000.0) * (2.0 * np.arange(16, dtype=np.float32) / 32)).astype(np.float32)
    ang = pos[None, :] * inv[:, None]
    c32 = np.concatenate([np.cos(ang), np.cos(ang)], 0)
    s32 = np.concatenate([-np.sin(ang), np.sin(ang)], 0)
    ropm = np.zeros((2, 96, PTOK), np.float32)
    ropm[0, 64:] = c32
    ropm[1, 64:] = s32
    return cm.astype(ml_dtypes.bfloat16), cf, ropd, ropm


def build_program(n_layers=2, n_tiles=len(TILES), debug=False, phases=4, max_ops=None):
    nc = bass.Bass("TRN2", target_bir_lowering=False)
    es = ExitStack()

    def din(name, shape, dt=F32):
        return nc.dram_tensor(name, list(shape), dt, kind="ExternalInput").ap()

    def dscr(name, shape, dt, out=False):
        kind = "ExternalOutput" if (out and debug) else "Internal"
        return nc.dram_tensor(name, list(shape), dt, kind=kind).ap()

    xp = din("xp", [PTOK, D])
    w_in = din("w_in", [2, D, IN_COLS])
    w_uq = din("mla_w_uq", [2, 256, 768])
    w_ukv = din("mla_w_ukv", [2, 128, 1024])
    w_br = din("w_branch", [2, 1536, D])
    w_out = din("w_out", [2, D, D])
    w_ff1 = din("w_ff1", [2, D, DFF])
    w_ff2 = din("w_ff2", [2, DFF, D])
    smallp_d = din("smallp", [128, 2 * SP_N])
    bcastp_d = din("bcastp", [128, 2 * 384])
    cm_d = din("cmat", [CM_N, 128, 128], BF16)
    cf_d = din("cfmat", [CF_N, 128, 128])
    ropd_d = din("ropd", [2, 128, PTOK])
    ropm_d = din("ropm", [2, 96, PTOK])
    out_d = nc.dram_tensor("out", [SEQ, D], F32, kind="ExternalOutput").ap()

    wb_in = dscr("wb_in", [2, D, IN_COLS], BF16)
    wb_uq = dscr("wb_uq", [2, 256, 768], BF16)
    wb_ukv = dscr("wb_ukv", [2, 128, 1024], BF16)
    wb_br = dscr("wb_br", [2, 1536, D], BF16)
    wb_out = dscr("wb_out", [2, D, D], BF16)
    wb_ff1 = dscr("wb_ff1", [2, D, DFF], BF16)
    wb_ff2 = dscr("wb_ff2", [2, DFF, D], BF16)
    kc_m = dscr("kc_m", [2, 8, 96, PTOK], BF16, out=True)
    kc_s = dscr("kc_s", [2, 4, 128, PTOK], BF16, out=True)
    kc_d = dscr("kc_d", [2, 4, 128, PTOK], BF16, out=True)
    vc_m = dscr("vc_m", [2, 128, NB, 8 * 65], BF16, out=True)
    vc_s = dscr("vc_s", [2, 128, NB, 512], BF16, out=True)
    vc_d = dscr("vc_d", [2, 128, NB, 4 * 129], BF16, out=True)
    hs = dscr("hs", [2, 128, 8, PTOK], F32, out=True)
    dbg_q = dscr("dbg_q", [3, 128, 8, 512], BF16, out=True) if debug else None
    dbg_o = dscr("dbg_o", [128, 12, 512], BF16, out=True) if debug else None

    kb = KB(nc, es)
    kb.max_ops = max_ops

    def sb(name, shape, dt):
        return es.enter_context(nc.sbuf_tensor(name, list(shape), dt))

    cmat = sb("cmat_sb", [128, CM_N, 128], BF16)
    cfm = sb("cfm_sb", [128, CF_N, 128], F32)
    u_const = Unit("const")
    smallp = sb("smallp_sb", [128, 2 * SP_N], F32)
    smalls = sb("smalls_sb", [128, 2 * SP_N], F32)
    bcastp = sb("bcastp_sb", [128, 2 * 384], F32)
    lamt = sb("lam_sb", [128, 2, 8], F32)
    gout = sb("gout_sb", [128, 2, 128], F32)
    ropd = [View(sb(f"ropd{i}", [128, 512], F32)[:, :]) for i in range(2)]
    ropm = [View(sb(f"ropm{i}", [128, 512], F32)[:, :]) for i in range(2)]
    hres_t = sb("hres", [128, 8, 512], F32)
    hres = [View(hres_t[:, i, :], name=f"hres{i}") for i in range(8)]
    hT_t = sb("hT", [128, 8, 512], BF16)
    hT = [View(hT_t[:, i, :], name=f"hT{i}") for i in range(8)]
    NWB = 3
    wblk = Ring([View(sb(f"wblk{i}", [128, 8, 512], BF16)[:, :, :], name=f"wblk{i}") for i in range(NWB)])
    wuq_t = sb("wuq", [128, 2, 768], BF16)
    wukv_t = sb("wukv", [128, 1024], BF16)
    wkpad_t = sb("wkpad", [128, 8, 96], BF16)
    wv_t = sb("wv", [128, 512], BF16)
    u_mlaw = Unit("mlaw")
    NF = 10
    F_t = sb("F", [128, NF, 512], F32)
    Fr = Ring([View(F_t[:, i, :], name=f"F{i}") for i in range(NF)])
    NBT = 13
    B_t = sb("B", [128, NBT, 512], BF16)
    Br = Ring([View(B_t[:, i, :], name=f"B{i}") for i in range(NBT)])
    X_t = sb("X", [128, 32, 512], BF16)
    X = [View(X_t[:, i, :], name=f"X{i}") for i in range(32)]
    Qm = X[0:8]
    Qs = X[8:16]
    Qd = X[16:24]
    cqn = X[24:26]
    ckvn = X[26]
    kr = X[27]
    mergedT = X[24:32]
    actT = X
    KBUF = [View(sb(f"kbuf{i}", [128, PTOK], BF16)[:, :], name=f"kbuf{i}") for i in range(2)]
    VBUF = [View(sb(f"vbuf{i}", [128, NB, 130], BF16)[:, :, :], name=f"vbuf{i}") for i in range(2)]
    Vst = Ring([View(sb(f"vst{i}", [128, 4, 520], BF16)[:, :, :], name=f"vst{i}") for i in range(2)])
    Ost_t = sb("Ost", [128, 4, 512], F32)
    Ost = [View(Ost_t[:, i, :], name=f"Ost{i}") for i in range(4)]
    OT_t = sb("OT", [128, 12, 512], BF16)
    OT = [View(OT_t[:, i, :], name=f"OT{i}") for i in range(12)]
    xio = Ring([View(sb(f"xio{i}", [128, 1024], F32)[:, :], name=f"xio{i}") for i in range(1)])
    carry32 = [View(sb(f"carry32_{i}", [128, 512], F32)[:, :], name=f"c32_{i}") for i in range(2)]
    carryb = [X[28], X[29]]
    sm_t = sb("sm", [128, 64], F32)
    sm = Ring([View(sm_t[:, 8 * i:8 * i + 8], name=f"sm{i}") for i in range(8)])

    onescol = View(sb("onescol", [128, 8], BF16)[:, :], name="onescol")
    u_out = Unit("out")
    PS = [View(es.enter_context(nc.psum_tensor(f"ps{i}", [128, 512], F32))[:, :], name=f"ps{i}") for i in range(8)]

    for v in PS:
        v.u.excl = True

    def cmv(i):
        return cmat[:, i, :]

    def cfv(i):
        return cfm[:, i, :]

    kb.dma("sp", cmat[:, :, :], cm_d.rearrange("c p n -> p c n"), [], [u_const])
    kb.dma("sp", cfm[:, :, :], cf_d.rearrange("c p n -> p c n"), [], [u_const])
    u_small = Unit("small")
    kb.dma("sp", smallp[:, :], smallp_d, [], [u_small])
    kb.dma("sp", bcastp[:, :], bcastp_d, [], [u_small])
    for l in range(2):
        o = l * SP_N

        def sc(c0, n, f):
            kb.ts("dve", smalls[:, o + c0:o + c0 + n], smallp[:, o + c0:o + c0 + n], float(f), None,
                  ALU.mult, None, [u_small], [u_small])
        sc(SP_LN1, 8, 32.0)
        sc(SP_LN2, 8, 32.0)
        sc(SP_CQ, 2, 16.0)
        sc(SP_CKV, 1, math.sqrt(128.0))
        sc(SP_MQ, 1, 1.0)
        sc(SP_MK, 1, math.sqrt(96.0))
        sc(SP_DQ, 1, 1.0)
        sc(SP_DK, 1, 8.0)
        sc(SP_GB, 24, -1.0)
        lam_init = 0.8 - 0.6 * math.exp(-0.3 * l)
        bo = l * 384
        prod = Fr.next()
        kb.tt("dve", prod.ap[:, 0:64], bcastp[:, bo + 128:bo + 192], bcastp[:, bo + 192:bo + 256], ALU.mult,
              [u_small], [prod])
        kb.tt("dve", prod.ap[:, 64:128], bcastp[:, bo + 256:bo + 320], bcastp[:, bo + 320:bo + 384], ALU.mult,
              [u_small], [prod])
        kb.op("dve", lambda e, l=l, prod=prod: e.reduce_sum(out=lamt[:, l, 2:3], in_=prod.ap[:, 0:64],
                                                           axis=mybir.AxisListType.X), [prod], [u_small])
        kb.op("dve", lambda e, l=l, prod=prod: e.reduce_sum(out=lamt[:, l, 3:4], in_=prod.ap[:, 64:128],
                                                           axis=mybir.AxisListType.X), [prod], [u_small])
        kb.act(lamt[:, l, 4:6], lamt[:, l, 2:4], AF.Exp, [u_small], [u_small])
        kb.tt("dve", lamt[:, l, 6:7], lamt[:, l, 4:5], lamt[:, l, 5:6], ALU.subtract, [u_small], [u_small])
        kb.ts("dve", lamt[:, l, 0:1], lamt[:, l, 6:7], float(lam_init), None, ALU.add, None, [u_small], [u_small])
        kb.ts("dve", lamt[:, l, 1:2], lamt[:, l, 0:1], -1.0, None, ALU.mult, None, [u_small], [u_small])
        kb.ts("dve", gout[:, l, :], bcastp[:, bo:bo + 128], float(1.0 - lam_init), None, ALU.mult, None,
              [u_small], [u_small])

    for v in KBUF:
        kb.memset("pool", v.ap[:, :], 0.0, [v])
    for v in X[0:30]:
        kb.memset("pool", v.ap[:, :], 0.0, [v])
    for v in carry32:
        kb.memset("pool", v.ap[:, :], 0.0, [v])
    for v in Vst.items:
        kb.memset("pool", v.ap[:, :, :], 1.0, [v])
    kb.memset("pool", onescol.ap[:, :], 1.0, [onescol])
    kb.memset("pool", onescol.ap[0:PAD, 1:2], 0.0, [onescol])

    cast_engs = Ring(["dve", "act"])

    u_wrow = {}

    def cast_jobs(name, l, src, dst, rows, cols):
        jobs = []
        for r0 in range(0, rows, 128):
            u = Unit(f"{name}{l}_{r0}")
            u_wrow[(name, l, r0 // 128)] = u

            def job(r0=r0, u=u):
                for c0 in range(0, cols, 1024):
                    cw = min(1024, cols - c0)
                    st = xio.next()
                    kb.dma("sp", st.ap[:, 0:cw], src[r0:r0 + 128, c0:c0 + cw], [], [st])
                    for h0 in range(0, cw, 512):
                        w0 = min(512, cw - h0)
                        bt = Br.next()
                        en = cast_engs.next()
                        kb.copy(en, bt.ap[:, 0:w0], st.ap[:, h0:h0 + w0], [st], [bt])
                        kb.dma("pool", dst[r0:r0 + 128, c0 + h0:c0 + h0 + w0], bt.ap[:, 0:w0], [bt], [], acc=[u])
            jobs.append(job)
        return jobs

    WSPEC = (("in", w_in, wb_in, D, IN_COLS), ("uq", w_uq, wb_uq, 256, 768), ("ukv", w_ukv, wb_ukv, 128, 1024),
             ("br", w_br, wb_br, 1536, D), ("out", w_out, wb_out, D, D), ("ff1", w_ff1, wb_ff1, D, DFF),
             ("ff2", w_ff2, wb_ff2, DFF, D))
    for (name, src, dst, rows, cols) in WSPEC:
        for job in cast_jobs(name, 0, src[0], dst[0], rows, cols):
            job()
    pending_cast = []
    if n_layers > 1:
        for (name, src, dst, rows, cols) in WSPEC:
            pending_cast.extend(cast_jobs(name, 1, src[1], dst[1], rows, cols))
    cast_per_tile = (len(pending_cast) + n_tiles - 1) // max(n_tiles, 1)

    def wunits(name, l, r0, r1):
        return [u_wrow[(name, l, r)] for r in range(r0, r1)]

    def load_wblk(name, dst_ap, l, k0, nk, c0, ncols):
        v = wblk.next()
        kb.dma("sp", v.ap[:, 0:nk, 0:ncols],
               dst_ap[l, k0 * 128:(k0 + nk) * 128, c0:c0 + ncols].rearrange("(k p) n -> p k n", p=128),
               wunits(name, l, k0, k0 + nk), [v])
        return v

    u_kc = {}
    u_vc = {}

    def ucache(d, key):
        if key not in d:
            d[key] = Unit(str(key))
        return d[key]

    u_hs = {}

    def rstd_from(ssq_ps, rows, n, dim):
        t = Fr.next()
        kb.act(t.ap[0:rows, 0:n], ssq_ps.ap[0:rows, 0:n], AF.Ln, [ssq_ps], [t], bias=float(EPS) * dim, scale=1.0)
        kb.act(t.ap[0:rows, 0:n], t.ap[0:rows, 0:n], AF.Exp, [t], [t], scale=-0.5)
        return t

    psd = Ring(PS[0:4])
    pss = Ring(PS[4:8])
    pst = psd

    def rmsnorm_stream(W, l, spcol):
        ssq = pss.next()
        for kc in range(8):
            sq = Fr.next()
            kb.act(sq.ap[:, 0:W], hres[kc].ap[:, 0:W], AF.Square, [hres[kc]], [sq])
            kb.mm(ssq.ap[:, 0:W], cfv(CF_ONES), sq.ap[:, 0:W], kc == 0, kc == 7, [sq, u_const], [ssq])
        r = rstd_from(ssq, 128, W, 1024)
        o = l * SP_N + spcol
        for kc in range(8):
            kb.stt(hT[kc].ap[:, 0:W], hres[kc].ap[:, 0:W], smalls[:, o + kc:o + kc + 1], r.ap[:, 0:W],
                   ALU.mult, ALU.mult, [hres[kc], r, u_small], [hT[kc]])

    def proj_fm(ps, blkv, j0, W, nk, rhs_views, reads_extra=()):
        for k in range(nk):
            kb.mm(ps.ap[:, 0:W], blkv.ap[:, k, j0:j0 + 128], rhs_views[k].ap[:, 0:W], k == 0, k == nk - 1,
                  [blkv, rhs_views[k]] + list(reads_extra), [ps])

    nr_pending = []

    def nr_flush():
        while nr_pending:
            nr_pending.pop(0)()

    def norm_rope(ps, rows, W, dim_eps, gcol, ones_ap, rmat, cosv, sinv, rope_lo, out_views, out_rows_list,
                  after=None):
        sq = Fr.next()
        kb.act(sq.ap[0:rows, 0:W], ps.ap[0:rows, 0:W], AF.Square, [ps], [sq])
        ssq = pss.next()
        kb.mm(ssq.ap[0:rows, 0:W], ones_ap, sq.ap[0:rows, 0:W], True, True, [sq, u_const], [ssq])

        def stage_b():
            r = rstd_from(ssq, rows, W, dim_eps)
            xn = Fr.next()
            kb.stt(xn.ap[0:rows, 0:W], ps.ap[0:rows, 0:W], gcol, r.ap[0:rows, 0:W], ALU.mult, ALU.mult,
                   [ps, r, u_small], [xn])
            xb = Br.next()
            kb.copy("dve", xb.ap[0:rows, 0:W], xn.ap[0:rows, 0:W], [xn], [xb])
            rot = ssq
            kb.mm(rot.ap[0:rows, 0:W], rmat, xb.ap[0:rows, 0:W], True, True, [xb, u_const], [rot])
            t1 = Fr.next()
            kb.tt("pool", t1.ap[rope_lo:rows, 0:W], xn.ap[rope_lo:rows, 0:W], cosv.ap[rope_lo:rows, 0:W], ALU.mult,
                  [xn, cosv], [t1])
            t2 = Fr.next()
            kb.tt("dve", t2.ap[rope_lo:rows, 0:W], rot.ap[rope_lo:rows, 0:W], sinv.ap[rope_lo:rows, 0:W], ALU.mult,
                  [rot, sinv], [t2])
            for (ov, r0, r1) in out_rows_list:
                if r0 < rope_lo:
                    e = min(r1, rope_lo)
                    kb.copy("act", ov.ap[r0:e, 0:W], xn.ap[r0:e, 0:W], [xn], [ov])
                if r1 > rope_lo:
                    s_ = max(r0, rope_lo)
                    kb.tt("dve", ov.ap[s_:r1, 0:W], t1.ap[s_:r1, 0:W], t2.ap[s_:r1, 0:W], ALU.add, [t1, t2], [ov])
            if after is not None:
                after()
        prev = nr_pending[:]
        del nr_pending[:]
        nr_pending.append(stage_b)
        for f in prev:
            f()

    for l in range(n_layers):
        last_layer = (l == 1)
        spo = l * SP_N
        kb.dma("sp", wuq_t[:, :, :], wb_uq[l].rearrange("(k p) n -> p k n", p=128), wunits("uq", l, 0, 2), [u_mlaw])
        kb.dma("sp", wukv_t[:, :], wb_ukv[l], wunits("ukv", l, 0, 1), [u_mlaw])
        kb.memset("pool", wkpad_t[:, :, :], 0.0, [u_mlaw])
        kb.copy("pool", wkpad_t[:, :, 0:64], wukv_t[:, :].rearrange("p (h c) -> p h c", c=128)[:, :, 0:64],
                [u_mlaw], [u_mlaw])
        kb.copy("pool", wv_t[:, :].rearrange("p (h c) -> p h c", c=64),
                wukv_t[:, :].rearrange("p (h c) -> p h c", c=128)[:, :, 64:128], [u_mlaw], [u_mlaw])

        for ti in range(n_tiles):
            b0, nb = TILES[ti]
            W = nb * 128
            s0 = b0 * 128
            nkb = b0 + nb
            tile_needs_out = not (last_layer and ti == 0)

            if l == 0:
                for b in range(nb):
                    xin = xio.next()
                    kb.dma("sp", xin.ap[:, :], xp[s0 + b * 128:s0 + (b + 1) * 128, :], [], [xin])
                    for half in range(2):
                        ps = pst.next()
                        for j in range(4):
                            kc = half * 4 + j
                            kb.tr(ps.ap[:, j * 128:(j + 1) * 128], xin.ap[:, kc * 128:(kc + 1) * 128], cfv(CF_IDENT),
                                  [xin, u_const], [ps])
                        for j in range(4):
                            kc = half * 4 + j
                            kb.copy("dve" if j % 2 == 0 else "act", hres[kc].ap[:, b * 128:(b + 1) * 128],
                                    ps.ap[:, j * 128:(j + 1) * 128], [ps], [hres[kc]])
            else:
                kb.dma("sp", hres_t[:, :, 0:W], hs[l - 1][:, :, s0:s0 + W], [ucache(u_hs, (l - 1, ti))], hres)
            kb.dma("sp", ropd[0].ap[:, 0:W], ropd_d[0][:, s0:s0 + W], [], [ropd[0]])
            kb.dma("sp", ropd[1].ap[:, 0:W], ropd_d[1][:, s0:s0 + W], [], [ropd[1]])
            kb.dma("sp", ropm[0].ap[0:96, 0:W], ropm_d[0][:, s0:s0 + W], [], [ropm[0]])
            kb.dma("sp", ropm[1].ap[0:96, 0:W], ropm_d[1][:, s0:s0 + W], [], [ropm[1]])

            rmsnorm_stream(W, l, SP_LN1)

            blk = load_wblk("in", wb_in, l, 0, 8, 0, 512)
            pcq = [psd.next(), psd.next()]
            for j in range(2):
                proj_fm(pcq[j], blk, j * 128, W, 8, hT)
            ssq = pss.next()
            for j in range(2):
                sq = Fr.next()
                kb.act(sq.ap[:, 0:W], pcq[j].ap[:, 0:W], AF.Square, [pcq[j]], [sq])
                kb.mm(ssq.ap[:, 0:W], cfv(CF_ONES), sq.ap[:, 0:W], j == 0, j == 1, [sq, u_const], [ssq])
            r = rstd_from(ssq, 128, W, 256)
            for j in range(2):
                kb.stt(cqn[j].ap[:, 0:W], pcq[j].ap[:, 0:W], smalls[:, spo + SP_CQ + j:spo + SP_CQ + j + 1],
                       r.ap[:, 0:W], ALU.mult, ALU.mult, [pcq[j], r, u_small], [cqn[j]])
            pkv = psd.next()
            proj_fm(pkv, blk, 256, W, 8, hT)
            sq = Fr.next()
            kb.act(sq.ap[:, 0:W], pkv.ap[:, 0:W], AF.Square, [pkv], [sq])
            ssq = pss.next()
            kb.mm(ssq.ap[:, 0:W], cfv(CF_ONES), sq.ap[:, 0:W], True, True, [sq, u_const], [ssq])
            r = rstd_from(ssq, 128, W, 128)
            kb.stt(ckvn.ap[:, 0:W], pkv.ap[:, 0:W], smalls[:, spo + SP_CKV:spo + SP_CKV + 1], r.ap[:, 0:W],
                   ALU.mult, ALU.mult, [pkv, r, u_small], [ckvn])
            pkr = psd.next()
            proj_fm(pkr, blk, 384, W, 8, hT)
            kb.copy("dve", kr.ap[0:32, 0:W], pkr.ap[0:32, 0:W], [pkr], [kr])

            for h in range(8):
                ps = psd.next()
                for j in range(2):
                    kb.mm(ps.ap[0:96, 0:W], wuq_t[:, j, h * 96:(h + 1) * 96], cqn[j].ap[:, 0:W], j == 0, j == 1,
                          [cqn[j], u_mlaw], [ps])
                norm_rope(ps, 96, W, 96, smalls[0:96, spo + SP_MQ:spo + SP_MQ + 1], cfm[0:96, CF_ONES, 0:96],
                          cmat[0:96, CM_RM, 0:96], ropm[0], ropm[1], 64, None, [(Qm[h], 0, 96)])
            for h in range(8):
                ps = psd.next()
                kb.mm(ps.ap[0:96, 0:W], wkpad_t[:, h, :], ckvn.ap[:, 0:W], True, False, [ckvn, u_mlaw], [ps])
                kb.mm(ps.ap[0:96, 0:W], cmat[:, CM_SEL, 0:96], kr.ap[:, 0:W], False, True, [kr, u_const], [ps])
                kst = Br.next()

                def _store_km(kst=kst, h=h):
                    kb.dma("pool", kc_m[l, h][:, s0:s0 + W], kst.ap[0:96, 0:W], [kst], [ucache(u_kc, ("m", l, h))])
                norm_rope(ps, 96, W, 96, smalls[0:96, spo + SP_MK:spo + SP_MK + 1], cfm[0:96, CF_ONES, 0:96],
                          cmat[0:96, CM_RM, 0:96], ropm[0], ropm[1], 64, None, [(kst, 0, 96)], after=_store_km)
            nr_flush()
            vst = Vst.next()
            for b in range(nb):
                ps = psd.next()
                kb.mm(ps.ap[:, :], ckvn.ap[:, b * 128:(b + 1) * 128], wv_t[:, :], True, True, [ckvn, u_mlaw], [ps])
                kb.copy("act", vst.ap[:, b, :].rearrange("p (h c) -> p h c", c=65)[:, :, 0:64],
                        ps.ap[:, :].rearrange("p (h c) -> p h c", c=64), [ps], [vst])
            if ti == 0:
                kb.memset("pool", vst.ap[0:PAD, 0, :], 0.0, [vst])
            kb.dma("pool", vc_m[l][:, b0:b0 + nb, :], vst.ap[:, 0:nb, :], [vst], [ucache(u_vc, ("m", l))])
            kb.memset("pool", vst.ap[:, :, :], 1.0, [vst])

            blk = load_wblk("in", wb_in, l, 0, 8, C_SBQ, 512)
            for pr in range(4):
                ps = psd.next()
                proj_fm(ps, blk, pr * 128, W, 8, hT)
                kb.ts("dve", Qs[2 * pr].ap[0:64, 0:W], ps.ap[0:64, 0:W], 0.125, None, ALU.mult, None, [ps], [Qs[2 * pr]])
                kb.act(Qs[2 * pr + 1].ap[64:128, 0:W], ps.ap[64:128, 0:W], AF.Copy, [ps], [Qs[2 * pr + 1]], scale=0.125)
            blk = load_wblk("in", wb_in, l, 0, 8, C_SBK, 512)
            for pr in range(4):
                ps = psd.next()
                proj_fm(ps, blk, pr * 128, W, 8, hT)
                kst = Br.next()
                kb.copy("dve" if pr % 2 else "act", kst.ap[:, 0:W], ps.ap[:, 0:W], [ps], [kst])
                kb.dma("pool", kc_s[l, pr][:, s0:s0 + W], kst.ap[:, 0:W], [kst], [ucache(u_kc, ("s", l, pr))])
            blk = load_wblk("in", wb_in, l, 0, 8, C_SBV, 512)
            vst = Vst.next()
            for b in range(nb):
                ps = psd.next()
                for kc in range(8):
                    kb.mm(ps.ap[:, :], hT[kc].ap[:, b * 128:(b + 1) * 128], blk.ap[:, kc, :], kc == 0, kc == 7,
                          [hT[kc], blk], [ps])
                kb.copy("act" if b % 2 else "dve", vst.ap[:, b, 0:512], ps.ap[:, :], [ps], [vst])
            if ti == 0:
                kb.memset("pool", vst.ap[0:PAD, 0, :], 0.0, [vst])
            kb.dma("pool", vc_s[l][:, b0:b0 + nb, :], vst.ap[:, 0:nb, 0:512], [vst], [ucache(u_vc, ("s", l))])
            kb.memset("pool", vst.ap[:, :, :], 1.0, [vst])

            for (c0, gcol, isq) in ((C_DFQ, SP_DQ, True), (C_DFK, SP_DK, False)):
                blk = load_wblk("in", wb_in, l, 0, 8, c0, 512)
                for h in range(4):
                    ps = psd.next()
                    proj_fm(ps, blk, h * 128, W, 8, hT)
                    if isq:
                        outs = [(Qd[2 * h], 0, 64), (Qd[2 * h + 1], 64, 128)]
                        kst = None
                    else:
                        kst = Br.next()
                        outs = [(kst, 0, 128)]
                    aft = None
                    if not isq:
                        def aft(kst=kst, h=h):
                            kb.dma("pool", kc_d[l, h][:, s0:s0 + W], kst.ap[:, 0:W], [kst], [ucache(u_kc, ("d", l, h))])
                    norm_rope(ps, 128, W, 64, smalls[:, spo + gcol:spo + gcol + 1], cfv(CF_BONES),
                              cmv(CM_RD), ropd[0], ropd[1], 0, None, outs, after=aft)
            nr_flush()
            blk = load_wblk("in", wb_in, l, 0, 8, C_DFV, 512)
            vst = Vst.next()
            for b in range(nb):
                ps = psd.next()
                for kc in range(8):
                    kb.mm(ps.ap[:, :], hT[kc].ap[:, b * 128:(b + 1) * 128], blk.ap[:, kc, :], kc == 0, kc == 7,
                          [hT[kc], blk], [ps])
                kb.copy("act" if b % 2 else "dve", vst.ap[:, b, 0:516].rearrange("p (h c) -> p h c", c=129)[:, :, 0:128],
                        ps.ap[:, :].rearrange("p (h c) -> p h c", c=128), [ps], [vst])
            if ti == 0:
                kb.memset("pool", vst.ap[0:PAD, 0, :], 0.0, [vst])
            kb.dma("pool", vc_d[l][:, b0:b0 + nb, :], vst.ap[:, 0:nb, 0:516], [vst], [ucache(u_vc, ("d", l))])
            kb.memset("pool", vst.ap[:, :, :], 1.0, [vst])

            if debug and l == 0 and ti == DBG_TI:
                for i, Q in enumerate((Qm, Qs, Qd)):
                    for j in range(8):
                        kb.dma("pool", dbg_q[i][:, j, 0:W], Q[j].ap[:, 0:W], [Q[j]], [])
            if l == 0 and pending_cast:
                lim = len(pending_cast) if ti == n_tiles - 1 else cast_per_tile
                for job in pending_cast[:lim]:
                    job()
                del pending_cast[:lim]
            if phases < 2 or not tile_needs_out:
                continue

            pS = Ring(PS[0:3])
            pO = Ring(PS[3:5])
            pX = Ring(PS[5:8])

            def transpose_branch(br):
                for c in range(4):
                    ps = pX.next()
                    for qb in range(nb):
                        kb.tr(ps.ap[:, qb * 128:(qb + 1) * 128], Ost[qb].ap[:, c * 128:(c + 1) * 128], cfv(CF_IDENT),
                              [Ost[qb], u_const], [ps])
                    kb.copy("act" if c % 2 else "dve", OT[br * 4 + c].ap[:, 0:W], ps.ap[:, 0:W], [ps], [OT[br * 4 + c]])

            def kcols(kbi):
                if kbi < b0:
                    return 0, False
                return (kbi - b0) * 128, True

            for hg in (range(4) if ATT_BR & 1 else []):
                vb = VBUF[hg % 2]
                kb.dma("sp", vb.ap[:, 0:nkb, 0:130], vc_m[l][:, 0:nkb, hg * 130:(hg + 1) * 130],
                       [ucache(u_vc, ("m", l))], [vb])
                for hh in range(2):
                    h = hg * 2 + hh
                    kbuf = KBUF[h % 2]
                    kb.dma("sp", kbuf.ap[0:96, 0:nkb * 128], kc_m[l, h][:, 0:nkb * 128], [ucache(u_kc, ("m", l, h))], [kbuf])
                    po = pO.next()
                    pendq = []
                    LA = 2
                    for kbi in range(nkb + LA):
                        cur = None
                        pend = pendq.pop(0) if len(pendq) == LA or (kbi >= nkb and pendq) else None
                        if kbi < nkb:
                            c0, diag = kcols(kbi)
                            ps = pS.next()
                            kb.mm(ps.ap[:, c0:W], kbuf.ap[:, kbi * 128:(kbi + 1) * 128], Qm[h].ap[:, c0:W], True, True,
                                  [kbuf, Qm[h]], [ps])
                            pt = Br.next()
                            kb.act(pt.ap[:, c0:W], ps.ap[:, c0:W], AF.Exp, [ps], [pt])
                            if diag:
                                kb.tt("dve", pt.ap[:, c0:c0 + 128], pt.ap[:, c0:c0 + 128], cmv(CM_MINCL), ALU.mult,
                                      [pt, u_const], [pt])
                            cur = (kbi, c0, pt)
                        if pend is not None:
                            pk, pc0, ppt = pend
                            for qb in range(pc0 // 128, nb):
                                kb.mm(po.ap[:, qb * 65:(qb + 1) * 65], ppt.ap[:, qb * 128:(qb + 1) * 128],
                                      vb.ap[:, pk, hh * 65:(hh + 1) * 65], pk == 0 and qb == pc0 // 128, pk == b0 + qb,
                                      [ppt, vb], [po], sig=(qb == nb - 1))
                        if cur is not None:
                            pendq.append(cur)
                    rc = sm.next()
                    pov = po.ap[:, 0:nb * 65].rearrange("p (q c) -> p q c", c=65)
                    kb.ts("dve", rc.ap[:, 0:nb], pov[:, :, 64], 1e-30, None, ALU.max, None, [po], [rc])
                    kb.recip(rc.ap[:, 0:nb], rc.ap[:, 0:nb], [rc], [rc])
                    for qb in range(nb):
                        kb.ts("dve", Ost[qb].ap[:, h * 64:(h + 1) * 64], po.ap[:, qb * 65:qb * 65 + 64], rc.ap[:, qb:qb + 1],
                              None, ALU.mult, None, [po, rc], [Ost[qb]])
            transpose_branch(0)

            pAb = [[PS[0], PS[1]], [PS[2], PS[5]]]
            pC = [PS[6], PS[7]]
            for hg in (range(4) if ATT_BR & 2 else []):
                vb = VBUF[hg % 2]
                kb.dma("sp", vb.ap[:, 0:nkb, 0:128], vc_s[l][:, 0:nkb, hg * 128:(hg + 1) * 128],
                       [ucache(u_vc, ("s", l))], [vb])
                pr = hg
                kbuf = KBUF[pr % 2]
                kb.dma("sp", kbuf.ap[:, 0:nkb * 128], kc_s[l, pr][:, 0:nkb * 128], [ucache(u_kc, ("s", l, pr))], [kbuf])
                po = pO.next()
                for s in range(2):
                    kb.memset("pool", carry32[s].ap[0:65, :], 0.0, [carry32[s]])
                    kb.memset("pool", carryb[s].ap[0:65, :], 0.0, [carryb[s]])
                kbs = list(range(nkb - 1, -1, -1))
                nit = len(kbs)
                st = {}

                def sb_s1(i):
                    kbi = kbs[i]
                    c0, diag = kcols(kbi)
                    sps = []
                    Es = []
                    for s in range(2):
                        pa = pAb[s][i % 2]
                        q = Qs[2 * pr + s]
                        kb.mm(pa.ap[:, c0:W], kbuf.ap[:, kbi * 128:(kbi + 1) * 128], q.ap[:, c0:W], True, True,
                              [kbuf, q], [pa])
                        E = Fr.next()
                        kb.act(E.ap[:, c0:W], pa.ap[:, c0:W], AF.Exp, [pa], [E])
                        Es.append(E)
                    for s in range(2):
                        E = Es[s]
                        sp = Br.next()
                        kb.act(sp.ap[:, c0:W], E.ap[:, c0:W], AF.Ln, [E], [sp], bias=1.0, scale=1.0)
                        if diag:
                            kb.tt("dve", sp.ap[:, c0:c0 + 128], sp.ap[:, c0:c0 + 128], cmv(CM_MSTRICT), ALU.mult,
                                  [sp, u_const], [sp])
                        if kbi == 0:
                            kb.memset("pool", sp.ap[0:PAD, c0:W], 0.0, [sp])
                        sps.append(sp)
                    st[i] = {"c0": c0, "diag": diag, "sps": sps, "kbi": kbi}

                def sb_s2(i):
                    c0, sps, kbi = st[i]["c0"], st[i]["sps"], st[i]["kbi"]
                    for s in range(2):
                        pa = pAb[s][i % 2]
                        kb.mm(pa.ap[:, c0:W], cmv(CM_NEGTRI), sps[s].ap[:, c0:W], False, False,
                              [sps[s], u_const], [pa])
                        kb.mm(pa.ap[:, c0:W], cmv(CM_CARRYL), carryb[s].ap[:, c0:W], False, True,
                              [carryb[s], u_const], [pa])
                        if kbi > 0:
                            kb.mm(pC[s].ap[:, c0:W], cmv(CM_NEGCOL), sps[s].ap[:, c0:W], True, True,
                                  [sps[s], u_const], [pC[s]])

                def sb_s3(i):
                    c0, diag, kbi = st[i]["c0"], st[i]["diag"], st[i]["kbi"]
                    pts = []
                    for s in range(2):
                        pa = pAb[s][i % 2]
                        pt = Br.next()
                        kb.act(pt.ap[:, c0:W], pa.ap[:, c0:W], AF.Exp, [pa], [pt])
                        if diag:
                            kb.tt("dve", pt.ap[:, c0:c0 + 128], pt.ap[:, c0:c0 + 128], cmv(CM_MSTRICT), ALU.mult,
                                  [pt, u_const], [pt])
                        pts.append(pt)
                    if kbi > 0:
                        for s in range(2):
                            kb.tt("dve", carry32[s].ap[0:65, c0:W], carry32[s].ap[0:65, c0:W], pC[s].ap[0:65, c0:W],
                                  ALU.add, [carry32[s], pC[s]], [carry32[s]])
                        for s in range(2):
                            kb.copy("dve", carryb[s].ap[0:65, c0:W], carry32[s].ap[0:65, c0:W], [carry32[s]], [carryb[s]])
                        for s in range(2):
                            kb.tt("pool", carryb[s].ap[0:1, c0:W], carry32[s].ap[0:1, c0:W], carryb[s].ap[0:1, c0:W],
                                  ALU.subtract, [carry32[s], carryb[s]], [carryb[s]])
                    st[i]["pts"] = pts

                def sb_s4(i):
                    c0, kbi, pts = st[i]["c0"], st[i]["kbi"], st[i]["pts"]
                    for s in range(2):
                        for qb in range(c0 // 128, nb):
                            kb.mm(po.ap[:, s * 256 + qb * 64:s * 256 + (qb + 1) * 64], pts[s].ap[:, qb * 128:(qb + 1) * 128],
                                  vb.ap[:, kbi, s * 64:(s + 1) * 64], i == 0 and s == 0 and qb == c0 // 128,
                                  kbi == 0, [pts[s], vb], [po], sig=(qb == nb - 1))
                    del st[i]

                for j in range(nit + 2):
                    if j < nit:
                        sb_s1(j)
                    if 0 <= j - 1 < nit:
                        sb_s2(j - 1)
                        sb_s3(j - 1)
                    if 0 <= j - 2 < nit:
                        sb_s4(j - 2)
                for s in range(2):
                    h = pr * 2 + s
                    for qb in range(nb):
                        kb.copy("dve", Ost[qb].ap[:, h * 64:(h + 1) * 64],
                                po.ap[:, s * 256 + qb * 64:s * 256 + (qb + 1) * 64], [po], [Ost[qb]])
            transpose_branch(1)

            pS4 = Ring([PS[0], PS[1], PS[2], PS[5]])
            pOd = [PS[3], PS[4]]
            pOx = PS[6]

            def oreg(m, qb, lo, hi):
                if qb < 3:
                    return pOd[m], pOd[m].ap[:, qb * 129 + lo:qb * 129 + hi]
                return pOx, pOx.ap[:, m * 129 + lo:m * 129 + hi]
            for hg in (range(4) if ATT_BR & 4 else []):
                vb = VBUF[hg % 2]
                kb.dma("sp", vb.ap[:, 0:nkb, 0:129], vc_d[l][:, 0:nkb, hg * 129:(hg + 1) * 129],
                       [ucache(u_vc, ("d", l))], [vb])
                h = hg
                kbuf = KBUF[h % 2]
                kb.dma("sp", kbuf.ap[:, 0:nkb * 128], kc_d[l, h][:, 0:nkb * 128], [ucache(u_kc, ("d", l, h))], [kbuf])
                pend = None
                for kbi in range(nkb + 1):
                    cur = None
                    if kbi < nkb:
                        c0, diag = kcols(kbi)
                        pts = []
                        for m in range(2):
                            ps = pS4.next()
                            q = Qd[2 * h + m]
                            kb.mm(ps.ap[:, c0:W], kbuf.ap[:, kbi * 128:(kbi + 1) * 128], q.ap[:, c0:W], True, True,
                                  [kbuf, q], [ps])
                            pt = Br.next()
                            kb.act(pt.ap[:, c0:W], ps.ap[:, c0:W], AF.Exp, [ps], [pt])
                            if diag:
                                kb.tt("dve", pt.ap[:, c0:c0 + 128], pt.ap[:, c0:c0 + 128], cmv(CM_MINCL), ALU.mult,
                                      [pt, u_const], [pt])
                            pts.append(pt)
                        cur = (kbi, c0, pts)
                    if pend is not None:
                        pk, pc0, ppts = pend
                        for m in range(2):
                            for qb in range(pc0 // 128, nb):
                                bank, oap = oreg(m, qb, 0, 129)
                                first = (pk == 0 and qb == 0) if qb < 3 else (pk == 0 and m == 0)
                                kb.mm(oap, ppts[m].ap[:, qb * 128:(qb + 1) * 128], vb.ap[:, pk, 0:129], first, pk == b0 + qb,
                                      [ppts[m], vb], [bank], sig=(qb == nb - 1))
                    pend = cur
                rc = sm.next()
                for m in range(2):
                    nq = min(nb, 3)
                    kb.ts("dve", rc.ap[:, m * 4:m * 4 + nq],
                          pOd[m].ap[:, 0:nq * 129].rearrange("p (q c) -> p q c", c=129)[:, :, 128], 1e-30, None,
                          ALU.max, None, [pOd[m]], [rc])
                    if nb == 4:
                        kb.ts("dve", rc.ap[:, m * 4 + 3:m * 4 + 4], pOx.ap[:, m * 129 + 128:m * 129 + 129], 1e-30, None,
                              ALU.max, None, [pOx], [rc])
                    kb.recip(rc.ap[:, m * 4:m * 4 + nb], rc.ap[:, m * 4:m * 4 + nb], [rc], [rc])
                kb.ts("dve", rc.ap[:, 4:4 + nb], rc.ap[:, 4:4 + nb], lamt[:, l, 1:2], None, ALU.mult, None, [rc, u_small], [rc])
                for qb in range(nb):
                    o1 = Fr.next()
                    bk0, oa0 = oreg(0, qb, 0, 128)
                    bk1, oa1 = oreg(1, qb, 0, 128)
                    kb.ts("dve", o1.ap[:, 0:128], oa0, rc.ap[:, qb:qb + 1], None, ALU.mult, None, [bk0, rc], [o1])
                    kb.stt(o1.ap[:, 128:256], oa1, rc.ap[:, 4 + qb:5 + qb], o1.ap[:, 0:128], ALU.mult, ALU.add,
                           [bk1, rc, o1], [o1])
                    ss = sm.next()
                    kb.act(o1.ap[:, 256:384], o1.ap[:, 128:256], AF.Square, [o1], [o1, ss], accum_out=ss.ap[:, 0:1])
                    kb.act(ss.ap[:, 1:2], ss.ap[:, 0:1], AF.Ln, [ss], [ss], bias=float(EPS), scale=1.0 / 128.0)
                    kb.act(ss.ap[:, 2:3], ss.ap[:, 1:2], AF.Exp, [ss], [ss], scale=-0.5)
                    kb.stt(Ost[qb].ap[:, h * 128:(h + 1) * 128], o1.ap[:, 128:256], ss.ap[:, 2:3], gout[:, l, :],
                           ALU.mult, ALU.mult, [o1, ss, u_small], [Ost[qb]])
            transpose_branch(2)

            if debug and l == 0 and ti == DBG_TI:
                kb.dma("pool", dbg_o[:, :, 0:W], OT_t[:, :, 0:W], OT, [])
            if phases < 3:
                continue

            psg = Ring(PS[0:3])
            psy = Ring(PS[3:6])
            pso = Ring(PS[6:8])
            for half in range(2):
                macc = Ost
                for br in range(3):
                    gblk = load_wblk("in", wb_in, l, 0, 8, C_GATE + br * 1024 + half * 512, 512)
                    bblk = load_wblk("br", wb_br, l, br * 4, 4, half * 512, 512)
                    for j in range(4):
                        cb = half * 4 + j
                        pg = psg.next()
                        proj_fm(pg, gblk, j * 128, W, 8, hT)
                        py = psy.next()
                        proj_fm(py, bblk, j * 128, W, 4, OT[br * 4:br * 4 + 4])
                        g = Fr.next()
                        gc = spo + SP_GB + br * 8 + cb
                        kb.act(g.ap[:, 0:W], pg.ap[:, 0:W], AF.Sigmoid, [pg, u_small], [g], bias=smallp[:, gc:gc + 1], scale=1.0)
                        if br == 0:
                            kb.tt("dve", macc[j].ap[:, 0:W], py.ap[:, 0:W], g.ap[:, 0:W], ALU.mult, [py, g], [macc[j]])
                        else:
                            kb.tt("dve", g.ap[:, 0:W], py.ap[:, 0:W], g.ap[:, 0:W], ALU.mult, [py, g], [g])
                            if br == 1:
                                kb.tt("pool", macc[j].ap[:, 0:W], macc[j].ap[:, 0:W], g.ap[:, 0:W], ALU.add, [macc[j], g], [macc[j]])
                            else:
                                kb.tt("pool", mergedT[cb].ap[:, 0:W], macc[j].ap[:, 0:W], g.ap[:, 0:W], ALU.add,
                                      [macc[j], g], [mergedT[cb]])
            for half in range(2):
                oblk = load_wblk("out", wb_out, l, 0, 8, half * 512, 512)
                for j in range(4):
                    cb = half * 4 + j
                    ps = pso.next()
                    proj_fm(ps, oblk, j * 128, W, 8, mergedT)
                    kb.tt("dve", hres[cb].ap[:, 0:W], hres[cb].ap[:, 0:W], ps.ap[:, 0:W], ALU.add, [hres[cb], ps], [hres[cb]])
            if phases < 4:
                kb.dma("pool", hs[l][:, :, s0:s0 + W], hres_t[:, :, 0:W], hres, [ucache(u_hs, (l, ti))])
                continue

            rmsnorm_stream(W, l, SP_LN2)
            psf = Ring(PS[0:4])
            for fb in range(8):
                blk = load_wblk("ff1", wb_ff1, l, 0, 8, fb * 512, 512)
                for j in range(4):
                    ps = psf.next()
                    proj_fm(ps, blk, j * 128, W, 8, hT)
                    rl = Fr.next()
                    kb.act(rl.ap[:, 0:W], ps.ap[:, 0:W], AF.Relu, [ps], [rl])
                    a = actT[fb * 4 + j]
                    kb.tt("pool" if j % 2 else "dve", a.ap[:, 0:W], rl.ap[:, 0:W], rl.ap[:, 0:W], ALU.mult, [rl], [a])
            for half in range(2):
                banks = PS[4:8]
                for g4 in range(4):
                    blk = load_wblk("ff2", wb_ff2, l, g4 * 8, 8, half * 512, 512)
                    for fc in range(8):
                        for j in range(4):
                            kb.mm(banks[j].ap[:, 0:W], blk.ap[:, fc, j * 128:(j + 1) * 128], actT[g4 * 8 + fc].ap[:, 0:W],
                                  g4 == 0 and fc == 0, g4 == 3 and fc == 7, [blk, actT[g4 * 8 + fc]], [banks[j]],
                                  sig=(j == 3))
                for j in range(4):
                    cb = half * 4 + j
                    kb.tt("dve", hres[cb].ap[:, 0:W], hres[cb].ap[:, 0:W], banks[j].ap[:, 0:W], ALU.add,
                          [hres[cb], banks[j]], [hres[cb]])
            for v in X[0:30]:
                kb.memset("pool", v.ap[:, :], 0.0, [v])

            if not last_layer or debug:
                kb.dma("pool", hs[l][:, :, s0:s0 + W], hres_t[:, :, 0:W], hres, [ucache(u_hs, (l, ti))])
            if last_layer:
                for b in range(nb):
                    xo = xio.next()
                    for half in range(2):
                        ps = PS[half]
                        for j in range(4):
                            kc = half * 4 + j
                            kb.tr(ps.ap[:, j * 128:(j + 1) * 128], hres[kc].ap[:, b * 128:(b + 1) * 128], cfv(CF_IDENT),
                                  [hres[kc], u_const], [ps])
                        kb.copy("act" if half else "dve", xo.ap[:, half * 512:(half + 1) * 512], ps.ap[:, :], [ps], [xo])
                    r0 = s0 + b * 128 - 128
                    kb.dma("sp", out_d[r0:r0 + 128, :], xo.ap[:, :], [xo], [u_out])

    kb.final_wait("sp", [u_out] + list(u_hs.values()) + list(u_kc.values()) + list(u_vc.values()))
    with nc.Block() as block:
        kb.emit(block)
    es.close()
    return nc, kb


def _host_inputs(x, meta_tokens, ln1_g, w_in, mla_cq_norm_g, mla_ckv_norm_g, mla_w_uq, mla_w_ukv,
                 mla_q_norm_g, mla_k_norm_g, diff_q_norm_g, diff_k_norm_g, diff_lambda,
                 diff_out_norm_g, gate_b, w_branch, w_out, ln2_g, w_ff1, w_ff2):
    f = np.float32
    smallp = np.zeros((128, 2 * SP_N), f)
    bcastp = np.zeros((128, 2 * 384), f)
    for l in range(2):
        o = l * SP_N
        smallp[:, o + SP_LN1:o + SP_LN1 + 8] = np.asarray(ln1_g[l], f).reshape(8, 128).T
        smallp[:, o + SP_LN2:o + SP_LN2 + 8] = np.asarray(ln2_g[l], f).reshape(8, 128).T
        smallp[:, o + SP_CQ:o + SP_CQ + 2] = np.asarray(mla_cq_norm_g[l], f).reshape(2, 128).T
        smallp[:, o + SP_CKV] = np.asarray(mla_ckv_norm_g[l], f)
        smallp[:96, o + SP_MQ] = np.asarray(mla_q_norm_g[l], f)
        smallp[:96, o + SP_MK] = np.asarray(mla_k_norm_g[l], f)
        smallp[:, o + SP_DQ] = np.tile(np.asarray(diff_q_norm_g[l], f), 2)
        smallp[:, o + SP_DK] = np.tile(np.asarray(diff_k_norm_g[l], f), 2)
        smallp[:, o + SP_GB:o + SP_GB + 24] = np.asarray(gate_b[l], f).reshape(24, 128).T
        bo = l * 384
        bcastp[:, bo:bo + 128] = np.asarray(diff_out_norm_g[l], f)[None, :]
        bcastp[:, bo + 128:bo + 384] = np.asarray(diff_lambda[l], f).reshape(1, 256)
    cm, cf, ropd, ropm = _consts()
    common = dict(w_in=np.ascontiguousarray(w_in, f), mla_w_uq=np.ascontiguousarray(mla_w_uq, f),
                  mla_w_ukv=np.ascontiguousarray(mla_w_ukv, f), w_branch=np.ascontiguousarray(w_branch, f),
                  w_out=np.ascontiguousarray(w_out, f), w_ff1=np.ascontiguousarray(w_ff1, f),
                  w_ff2=np.ascontiguousarray(w_ff2, f), smallp=smallp, bcastp=bcastp, cmat=cm, cfmat=cf,
                  ropd=ropd, ropm=ropm)
    maps = []
    x = np.asarray(x, f)
    meta = np.asarray(meta_tokens, f)
    for c in range(8):
        b = c % 4
        xp = np.concatenate([np.zeros((PAD, D), f), meta, x[b]], 0)
        m = dict(common)
        m["xp"] = np.ascontiguousarray(xp)
        maps.append(m)
    return maps


_PROG = None
import os as _os
DBG_TI = int(_os.environ.get('DBG_TI', '1'))
ATT_BR = int(_os.environ.get('ATT_BR', '7'))


def kernel(**inputs):
    global _PROG
    maps = _host_inputs(**inputs)
    if _PROG is None:
        _PROG = build_program()[0]
    res = run_bass_kernel_spmd(_PROG, maps, core_ids=list(range(8)))
    out = np.stack([np.asarray(res.results[c]["out"], np.float32) for c in range(4)], 0)
    return out
```

```python
import math
from contextlib import ExitStack

import numpy as np
import ml_dtypes

import concourse.bass as bass
import concourse.mybir as mybir
from concourse.bass_utils import run_bass_kernel_spmd

F32 = mybir.dt.float32
BF16 = mybir.dt.bfloat16
AF = mybir.ActivationFunctionType
ALU = mybir.AluOpType

D = 1024
SEQ = 4096
NB = 33
PTOK = NB * 128
PAD = 112
EPS = 1e-6
IN_COLS = 6560
DFF = 4096
C_CQ, C_CKV, C_KR, C_SBQ, C_SBK, C_SBV, C_DFQ, C_DFK, C_DFV, C_GATE = (
    0, 256, 384, 416, 928, 1440, 1952, 2464, 2976, 3488)
TILES = [(0, 1)] + [(1 + 4 * i, 4) for i in range(8)]

SP_LN1, SP_LN2, SP_CQ, SP_CKV, SP_MQ, SP_MK, SP_DQ, SP_DK, SP_GB = 0, 8, 16, 18, 19, 20, 21, 22, 23
SP_N = 47
CM_NEGTRI, CM_MINCL, CM_MSTRICT, CM_RD, CM_RM, CM_SEL, CM_NEGCOL, CM_CARRYL, CM_N = 0, 1, 2, 3, 4, 5, 6, 7, 8
CF_IDENT, CF_ONES, CF_BONES, CF_N = 0, 1, 2, 3


class Unit:
    __slots__ = ("w", "ws", "rs", "name", "excl")

    def __init__(self, name=""):
        self.excl = False
        self.w = None
        self.ws = []
        self.rs = []
        self.name = name


class View:
    __slots__ = ("ap", "u")

    def __init__(self, ap, u=None, name=""):
        self.ap = ap
        self.u = u if u is not None else Unit(name)


class Ring:
    def __init__(self, items):
        self.items = list(items)
        self.i = 0

    def next(self):
        v = self.items[self.i]
        self.i = (self.i + 1) % len(self.items)
        return v


class Eng:
    def __init__(self, name):
        self.name = name
        self.q = []
        self.n = 0
        self.seen = {}
        self.sem = None
        self.key = None


NDS = 12


class KB:
    def __init__(self, nc, es):
        self.nc = nc
        self.sems = {}
        self.engs = {}
        for n in ("pe", "act", "dve", "pool", "sp"):
            e = Eng(n)
            e.sem = es.enter_context(nc.semaphore("s_" + n))
            e.key = "s_" + n
            self.sems[e.key] = e.sem
            self.engs[n] = e
        self.dring = {}
        for q in ("sp", "pool"):
            keys = []
            for i in range(NDS):
                k = f"d_{q}{i}"
                self.sems[k] = es.enter_context(nc.semaphore(k))
                keys.append(k)
            self.dring[q] = {"keys": keys, "cnt": [0] * NDS, "next": 0}
        self.nwaits = 0
        self.nops = 0
        self.max_ops = None
        self.log = []

    def _collect(self, eng, reads, writes):
        toks = {}

        def add(t, war=False):
            if t is None:
                return
            k, v = t
            if k == eng.key:
                if eng.name == "pe" or (war and eng.name != "pool"):
                    return
            if toks.get(k, 0) < v:
                toks[k] = v
        for u in reads:
            add(u.w)
            for t in u.ws:
                add(t)
            if u.excl:
                for t in u.rs:
                    add(t, war=True)
        for u in writes:
            add(u.w)
            for t in u.ws:
                add(t)
            for t in u.rs:
                add(t, war=True)
        waits = []
        for k, v in toks.items():
            if eng.seen.get(k, 0) >= v:
                continue
            eng.seen[k] = v
            waits.append((k, v))
        return waits

    @staticmethod
    def _units(lst):
        out = []
        for x in lst:
            if isinstance(x, Unit):
                out.append(x)
            elif isinstance(x, View):
                out.append(x.u)
            else:
                out.extend(KB._units(x))
        return out

    def op(self, en, fn, reads=(), writes=(), sig=True):
        if self.max_ops is not None and self.nops >= self.max_ops:
            return
        eng = self.engs[en]
        if self.max_ops is not None:
            import sys as _s
            f = _s._getframe(1)
            while f.f_code.co_name in ("mm", "tr", "act", "tt", "ts", "stt", "copy", "memset", "recip", "op"):
                f = f.f_back
            self.log.append((self.nops, en, f.f_lineno, f.f_code.co_name))
        reads = self._units(reads)
        writes = self._units(writes)
        waits = self._collect(eng, reads, writes)
        tok = (eng.key, eng.n + 1)
        if sig:
            eng.n += 1
        eng.q.append((waits, fn, "inc" if sig else None, None))
        for u in reads:
            u.rs.append(tok)
        for u in writes:
            u.w = tok
            u.ws = []
            u.rs = []
        self.nops += 1
        self.nwaits += len(waits)

    def dma(self, q, out, in_, reads=(), writes=(), acc=(), **kw):
        if self.max_ops is not None and self.nops >= self.max_ops:
            return
        eng = self.engs[q]
        if self.max_ops is not None:
            import sys as _s
            f = _s._getframe(1)
            self.log.append((self.nops, "dma_" + q, f.f_lineno, f.f_code.co_name))
        ring = self.dring[q]
        i = ring["next"]
        ring["next"] = (i + 1) % NDS
        reads = self._units(reads)
        writes = self._units(writes)
        waits = self._collect(eng, reads, writes)
        k = ring["keys"][i]
        prev = ring["cnt"][i]
        if prev > 0 and eng.seen.get(k, 0) < prev:
            eng.seen[k] = prev
            waits.append((k, prev))
        ring["cnt"][i] = prev + 16
        tok = (k, prev + 16)
        eng.q.append((waits, (lambda e: e.dma_start(out=out, in_=in_, **kw)), "dma", k))
        for u in reads:
            u.rs.append(tok)
        for u in writes:
            u.w = tok
            u.ws = []
            u.rs = []
        for u in self._units(acc):
            u.ws.append(tok)
        self.nops += 1
        self.nwaits += len(waits)

    def final_wait(self, en, units):
        eng = self.engs[en]
        waits = self._collect(eng, self._units(units), [])
        eng.q.append((waits, None, None, None))

    def emit(self, block):
        def mk(eng):
            def f(e):
                for waits, fn, kind, k in eng.q:
                    for wk, wv in waits:
                        e.wait_ge(self.sems[wk], wv)
                    if fn is None:
                        continue
                    ins = fn(e)
                    if kind == "inc":
                        ins.then_inc(eng.sem, 1)
                    elif kind == "dma":
                        ins.then_inc(self.sems[k], 16)
            return f
        block.tensor(mk(self.engs["pe"]))
        block.scalar(mk(self.engs["act"]))
        block.vector(mk(self.engs["dve"]))
        block.gpsimd(mk(self.engs["pool"]))
        block.sync(mk(self.engs["sp"]))

    def mm(self, out, lhsT, rhs, start, stop, reads, writes, sig=True):
        self.op("pe", lambda e: e.matmul(out, lhsT=lhsT, rhs=rhs, start=start, stop=stop,
                                         skip_group_check=True), reads, writes, sig)

    def tr(self, out, in_, ident, reads, writes):
        self.op("pe", lambda e: e.matmul(out, lhsT=in_, rhs=ident, start=True, stop=True,
                                         skip_group_check=True), reads, writes)

    def act(self, out, in_, func, reads, writes, bias=None, scale=None, accum_out=None):
        kw = {}
        if bias is not None:
            kw["bias"] = bias
        if scale is not None:
            kw["scale"] = scale
        if accum_out is not None:
            kw["accum_out"] = accum_out
        self.op("act", lambda e: e.activation(out=out, in_=in_, func=func, **kw), reads, writes)

    def tt(self, en, out, in0, in1, op, reads, writes):
        self.op(en, lambda e: e.tensor_tensor(out=out, in0=in0, in1=in1, op=op), reads, writes)

    def ts(self, en, out, in0, s1, s2, op0, op1, reads, writes):
        if op1 is None:
            self.op(en, lambda e: e.tensor_scalar(out=out, in0=in0, scalar1=s1, scalar2=None, op0=op0),
                    reads, writes)
        else:
            self.op(en, lambda e: e.tensor_scalar(out=out, in0=in0, scalar1=s1, scalar2=s2, op0=op0, op1=op1),
                    reads, writes)

    def stt(self, out, in0, scalar, in1, op0, op1, reads, writes):
        self.op("dve", lambda e: e.scalar_tensor_tensor(out=out, in0=in0, scalar=scalar, in1=in1,
                                                        op0=op0, op1=op1), reads, writes)

    def copy(self, en, out, in_, reads, writes):
        if en == "act":
            self.op("act", lambda e: e.activation(out=out, in_=in_, func=AF.Copy), reads, writes)
        else:
            self.op(en, lambda e: e.tensor_copy(out=out, in_=in_), reads, writes)

    def memset(self, en, ap, val, writes):
        self.op(en, lambda e: e.memset(ap, val), (), writes)

    def recip(self, out, in_, reads, writes):
        self.op("dve", lambda e: e.reciprocal(out=out, in_=in_), reads, writes)


def _consts():
    cm = np.zeros((CM_N, 128, 128), np.float32)
    kp = np.arange(128)[:, None]
    kk = np.arange(128)[None, :]
    cm[CM_NEGTRI] = -(kp >= kk).astype(np.float32)
    cm[CM_MINCL] = (kp <= kk).astype(np.float32)
    cm[CM_MSTRICT] = (kp < kk).astype(np.float32)
    for d in range(128):
        g = (d // 64) * 64
        cm[CM_RD, g + ((d - g + 32) % 64), d] = 1.0
    for d in range(64, 96):
        cm[CM_RM, 64 + ((d - 64 + 16) % 32), d] = 1.0
    for r in range(32):
        cm[CM_SEL, r, 64 + r] = 1.0
    cm[CM_NEGCOL][:, 0] = -1.0
    cm[CM_NEGCOL][:, 64] = -1.0
    cm[CM_CARRYL][0, :] = 1.0
    cm[CM_CARRYL][64, :] = 1.0
    cf = np.zeros((CF_N, 128, 128), np.float32)
    cf[CF_IDENT] = np.eye(128, dtype=np.float32)
    cf[CF_ONES] = 1.0
    cf[CF_BONES][:64, :64] = 1.0
    cf[CF_BONES][64:, 64:] = 1.0
    pos = np.maximum(np.arange(PTOK) - PAD, 0).astype(np.float32)
    inv = np.exp(-math.log(10000.0) * (2.0 * np.arange(32, dtype=np.float32) / 64)).astype(np.float32)
    ang = pos[None, :] * inv[:, None]
    c64 = np.concatenate([np.cos(ang), np.cos(ang)], 0)
    s64 = np.concatenate([-np.sin(ang), np.sin(ang)], 0)
    ropd = np.stack([np.concatenate([c64, c64], 0), np.concatenate([s64, s64], 0)], 0).astype(np.float32)
    inv = np.exp(-math.log(10000.0) * (2.0 * np.arange(16, dtype=np.float32) / 32)).astype(np.float32)
    ang = pos[None, :] * inv[:, None]
    c32 = np.concatenate([np.cos(ang), np.cos(ang)], 0)
    s32 = np.concatenate([-np.sin(ang), np.sin(ang)], 0)
    ropm = np.zeros((2, 96, PTOK), np.float32)
    ropm[0, 64:] = c32
    ropm[1, 64:] = s32
    return cm.astype(ml_dtypes.bfloat16), cf, ropd, ropm


def build_program(n_layers=2, n_tiles=len(TILES), debug=False, phases=4, max_ops=None):
    nc = bass.Bass("TRN2", target_bir_lowering=False)
    es = ExitStack()

    def din(name, shape, dt=F32):
        return nc.dram_tensor(name, list(shape), dt, kind="ExternalInput").ap()

    def dscr(name, shape, dt, out=False):
        kind = "ExternalOutput" if (out and debug) else "Internal"
        return nc.dram_tensor(name, list(shape), dt, kind=kind).ap()

    xp = din("xp", [PTOK, D])
    w_in = din("w_in", [2, D, IN_COLS])
    w_uq = din("mla_w_uq", [2, 256, 768])
    w_ukv = din("mla_w_ukv", [2, 128, 1024])
    w_br = din("w_branch", [2, 1536, D])
    w_out = din("w_out", [2, D, D])
    w_ff1 = din("w_ff1", [2, D, DFF])
    w_ff2 = din("w_ff2", [2, DFF, D])
    smallp_d = din("smallp", [128, 2 * SP_N])
    bcastp_d = din("bcastp", [128, 2 * 384])
    cm_d = din("cmat", [CM_N, 128, 128], BF16)
    cf_d = din("cfmat", [CF_N, 128, 128])
    ropd_d = din("ropd", [2, 128, PTOK])
    ropm_d = din("ropm", [2, 96, PTOK])
    out_d = nc.dram_tensor("out", [SEQ, D], F32, kind="ExternalOutput").ap()

    wb_in = dscr("wb_in", [2, D, IN_COLS], BF16)
    wb_uq = dscr("wb_uq", [2, 256, 768], BF16)
    wb_ukv = dscr("wb_ukv", [2, 128, 1024], BF16)
    wb_br = dscr("wb_br", [2, 1536, D], BF16)
    wb_out = dscr("wb_out", [2, D, D], BF16)
    wb_ff1 = dscr("wb_ff1", [2, D, DFF], BF16)
    wb_ff2 = dscr("wb_ff2", [2, DFF, D], BF16)
    kc_m = dscr("kc_m", [2, 8, 96, PTOK], BF16, out=True)
    kc_s = dscr("kc_s", [2, 4, 128, PTOK], BF16, out=True)
    kc_d = dscr("kc_d", [2, 4, 128, PTOK], BF16, out=True)
    vc_m = dscr("vc_m", [2, 128, NB, 8 * 65], BF16, out=True)
    vc_s = dscr("vc_s", [2, 128, NB, 512], BF16, out=True)
    vc_d = dscr("vc_d", [2, 128, NB, 4 * 129], BF16, out=True)
    hs = dscr("hs", [2, 128, 8, PTOK], F32, out=True)
    dbg_q = dscr("dbg_q", [3, 128, 8, 512], BF16, out=True) if debug else None
    dbg_o = dscr("dbg_o", [128, 12, 512], BF16, out=True) if debug else None

    kb = KB(nc, es)
    kb.max_ops = max_ops

    def sb(name, shape, dt):
        return es.enter_context(nc.sbuf_tensor(name, list(shape), dt))

    cmat = sb("cmat_sb", [128, CM_N, 128], BF16)
    cfm = sb("cfm_sb", [128, CF_N, 128], F32)
    u_const = Unit("const")
    smallp = sb("smallp_sb", [128, 2 * SP_N], F32)
    smalls = sb("smalls_sb", [128, 2 * SP_N], F32)
    bcastp = sb("bcastp_sb", [128, 2 * 384], F32)
    lamt = sb("lam_sb", [128, 2, 8], F32)
    gout = sb("gout_sb", [128, 2, 128], F32)
    ropd = [View(sb(f"ropd{i}", [128, 512], F32)[:, :]) for i in range(2)]
    ropm = [View(sb(f"ropm{i}", [128, 512], F32)[:, :]) for i in range(2)]
    hres_t = sb("hres", [128, 8, 512], F32)
    hres = [View(hres_t[:, i, :], name=f"hres{i}") for i in range(8)]
    hT_t = sb("hT", [128, 8, 512], BF16)
    hT = [View(hT_t[:, i, :], name=f"hT{i}") for i in range(8)]
    NWB = 3
    wblk = Ring([View(sb(f"wblk{i}", [128, 8, 512], BF16)[:, :, :], name=f"wblk{i}") for i in range(NWB)])
    wuq_t = sb("wuq", [128, 2, 768], BF16)
    wukv_t = sb("wukv", [128, 1024], BF16)
    wkpad_t = sb("wkpad", [128, 8, 96], BF16)
    wv_t = sb("wv", [128, 512], BF16)
    u_mlaw = Unit("mlaw")
    NF = 10
    F_t = sb("F", [128, NF, 512], F32)
    Fr = Ring([View(F_t[:, i, :], name=f"F{i}") for i in range(NF)])
    NBT = 13
    B_t = sb("B", [128, NBT, 512], BF16)
    Br = Ring([View(B_t[:, i, :], name=f"B{i}") for i in range(NBT)])
    X_t = sb("X", [128, 32, 512], BF16)
    X = [View(X_t[:, i, :], name=f"X{i}") for i in range(32)]
    Qm = X[0:8]
    Qs = X[8:16]
    Qd = X[16:24]
    cqn = X[24:26]
    ckvn = X[26]
    kr = X[27]
    mergedT = X[24:32]
    actT = X
    KBUF = [View(sb(f"kbuf{i}", [128, PTOK], BF16)[:, :], name=f"kbuf{i}") for i in range(2)]
    VBUF = [View(sb(f"vbuf{i}", [128, NB, 130], BF16)[:, :, :], name=f"vbuf{i}") for i in range(2)]
    Vst = Ring([View(sb(f"vst{i}", [128, 4, 520], BF16)[:, :, :], name=f"vst{i}") for i in range(2)])
    Ost_t = sb("Ost", [128, 4, 512], F32)
    Ost = [View(Ost_t[:, i, :], name=f"Ost{i}") for i in range(4)]
    OT_t = sb("OT", [128, 12, 512], BF16)
    OT = [View(OT_t[:, i, :], name=f"OT{i}") for i in range(12)]
    xio = Ring([View(sb(f"xio{i}", [128, 1024], F32)[:, :], name=f"xio{i}") for i in range(1)])
    carry32 = [View(sb(f"carry32_{i}", [128, 512], F32)[:, :], name=f"c32_{i}") for i in range(2)]
    carryb = [X[28], X[29]]
    sm_t = sb("sm", [128, 64], F32)
    sm = Ring([View(sm_t[:, 8 * i:8 * i + 8], name=f"sm{i}") for i in range(8)])

    onescol = View(sb("onescol", [128, 8], BF16)[:, :], name="onescol")
    u_out = Unit("out")
    PS = [View(es.enter_context(nc.psum_tensor(f"ps{i}", [128, 512], F32))[:, :], name=f"ps{i}") for i in range(8)]

    for v in PS:
        v.u.excl = True

    def cmv(i):
        return cmat[:, i, :]

    def cfv(i):
        return cfm[:, i, :]

    kb.dma("sp", cmat[:, :, :], cm_d.rearrange("c p n -> p c n"), [], [u_const])
    kb.dma("sp", cfm[:, :, :], cf_d.rearrange("c p n -> p c n"), [], [u_const])
    u_small = Unit("small")
    kb.dma("sp", smallp[:, :], smallp_d, [], [u_small])
    kb.dma("sp", bcastp[:, :], bcastp_d, [], [u_small])
    for l in range(2):
        o = l * SP_N

        def sc(c0, n, f):
            kb.ts("dve", smalls[:, o + c0:o + c0 + n], smallp[:, o + c0:o + c0 + n], float(f), None,
                  ALU.mult, None, [u_small], [u_small])
        sc(SP_LN1, 8, 32.0)
        sc(SP_LN2, 8, 32.0)
        sc(SP_CQ, 2, 16.0)
        sc(SP_CKV, 1, math.sqrt(128.0))
        sc(SP_MQ, 1, 1.0)
        sc(SP_MK, 1, math.sqrt(96.0))
        sc(SP_DQ, 1, 1.0)
        sc(SP_DK, 1, 8.0)
        sc(SP_GB, 24, -1.0)
        lam_init = 0.8 - 0.6 * math.exp(-0.3 * l)
        bo = l * 384
        prod = Fr.next()
        kb.tt("dve", prod.ap[:, 0:64], bcastp[:, bo + 128:bo + 192], bcastp[:, bo + 192:bo + 256], ALU.mult,
              [u_small], [prod])
        kb.tt("dve", prod.ap[:, 64:128], bcastp[:, bo + 256:bo + 320], bcastp[:, bo + 320:bo + 384], ALU.mult,
              [u_small], [prod])
        kb.op("dve", lambda e, l=l, prod=prod: e.reduce_sum(out=lamt[:, l, 2:3], in_=prod.ap[:, 0:64],
                                                           axis=mybir.AxisListType.X), [prod], [u_small])
        kb.op("dve", lambda e, l=l, prod=prod: e.reduce_sum(out=lamt[:, l, 3:4], in_=prod.ap[:, 64:128],
                                                           axis=mybir.AxisListType.X), [prod], [u_small])
        kb.act(lamt[:, l, 4:6], lamt[:, l, 2:4], AF.Exp, [u_small], [u_small])
        kb.tt("dve", lamt[:, l, 6:7], lamt[:, l, 4:5], lamt[:, l, 5:6], ALU.subtract, [u_small], [u_small])
        kb.ts("dve", lamt[:, l, 0:1], lamt[:, l, 6:7], float(lam_init), None, ALU.add, None, [u_small], [u_small])
        kb.ts("dve", lamt[:, l, 1:2], lamt[:, l, 0:1], -1.0, None, ALU.mult, None, [u_small], [u_small])
        kb.ts("dve", gout[:, l, :], bcastp[:, bo:bo + 128], float(1.0 - lam_init), None, ALU.mult, None,
              [u_small], [u_small])

    for v in KBUF:
        kb.memset("pool", v.ap[:, :], 0.0, [v])
    for v in X[0:30]:
        kb.memset("pool", v.ap[:, :], 0.0, [v])
    for v in carry32:
        kb.memset("pool", v.ap[:, :], 0.0, [v])
    for v in Vst.items:
        kb.memset("pool", v.ap[:, :, :], 1.0, [v])
    kb.memset("pool", onescol.ap[:, :], 1.0, [onescol])
    kb.memset("pool", onescol.ap[0:PAD, 1:2], 0.0, [onescol])

    cast_engs = Ring(["dve", "act"])

    u_wrow = {}

    def cast_jobs(name, l, src, dst, rows, cols):
        jobs = []
        for r0 in range(0, rows, 128):
            u = Unit(f"{name}{l}_{r0}")
            u_wrow[(name, l, r0 // 128)] = u

            def job(r0=r0, u=u):
                for c0 in range(0, cols, 1024):
                    cw = min(1024, cols - c0)
                    st = xio.next()
                    kb.dma("sp", st.ap[:, 0:cw], src[r0:r0 + 128, c0:c0 + cw], [], [st])
                    for h0 in range(0, cw, 512):
                        w0 = min(512, cw - h0)
                        bt = Br.next()
                        en = cast_engs.next()
                        kb.copy(en, bt.ap[:, 0:w0], st.ap[:, h0:h0 + w0], [st], [bt])
                        kb.dma("pool", dst[r0:r0 + 128, c0 + h0:c0 + h0 + w0], bt.ap[:, 0:w0], [bt], [], acc=[u])
            jobs.append(job)
        return jobs

    WSPEC = (("in", w_in, wb_in, D, IN_COLS), ("uq", w_uq, wb_uq, 256, 768), ("ukv", w_ukv, wb_ukv, 128, 1024),
             ("br", w_br, wb_br, 1536, D), ("out", w_out, wb_out, D, D), ("ff1", w_ff1, wb_ff1, D, DFF),
             ("ff2", w_ff2, wb_ff2, DFF, D))
    for (name, src, dst, rows, cols) in WSPEC:
        for job in cast_jobs(name, 0, src[0], dst[0], rows, cols):
            job()
    pending_cast = []
    if n_layers > 1:
        for (name, src, dst, rows, cols) in WSPEC:
            pending_cast.extend(cast_jobs(name, 1, src[1], dst[1], rows, cols))
    cast_per_tile = (len(pending_cast) + n_tiles - 1) // max(n_tiles, 1)

    def wunits(name, l, r0, r1):
        return [u_wrow[(name, l, r)] for r in range(r0, r1)]

    def load_wblk(name, dst_ap, l, k0, nk, c0, ncols):
        v = wblk.next()
        kb.dma("sp", v.ap[:, 0:nk, 0:ncols],
               dst_ap[l, k0 * 128:(k0 + nk) * 128, c0:c0 + ncols].rearrange("(k p) n -> p k n", p=128),
               wunits(name, l, k0, k0 + nk), [v])
        return v

    u_kc = {}
    u_vc = {}

    def ucache(d, key):
        if key not in d:
            d[key] = Unit(str(key))
        return d[key]

    u_hs = {}

    def rstd_from(ssq_ps, rows, n, dim):
        t = Fr.next()
        kb.act(t.ap[0:rows, 0:n], ssq_ps.ap[0:rows, 0:n], AF.Ln, [ssq_ps], [t], bias=float(EPS) * dim, scale=1.0)
        kb.act(t.ap[0:rows, 0:n], t.ap[0:rows, 0:n], AF.Exp, [t], [t], scale=-0.5)
        return t

    psd = Ring(PS[0:4])
    pss = Ring(PS[4:8])
    pst = psd

    def rmsnorm_stream(W, l, spcol):
        ssq = pss.next()
        for kc in range(8):
            sq = Fr.next()
            kb.act(sq.ap[:, 0:W], hres[kc].ap[:, 0:W], AF.Square, [hres[kc]], [sq])
            kb.mm(ssq.ap[:, 0:W], cfv(CF_ONES), sq.ap[:, 0:W], kc == 0, kc == 7, [sq, u_const], [ssq])
        r = rstd_from(ssq, 128, W, 1024)
        o = l * SP_N + spcol
        for kc in range(8):
            kb.stt(hT[kc].ap[:, 0:W], hres[kc].ap[:, 0:W], smalls[:, o + kc:o + kc + 1], r.ap[:, 0:W],
                   ALU.mult, ALU.mult, [hres[kc], r, u_small], [hT[kc]])

    def proj_fm(ps, blkv, j0, W, nk, rhs_views, reads_extra=()):
        for k in range(nk):
            kb.mm(ps.ap[:, 0:W], blkv.ap[:, k, j0:j0 + 128], rhs_views[k].ap[:, 0:W], k == 0, k == nk - 1,
                  [blkv, rhs_views[k]] + list(reads_extra), [ps])

    nr_pending = []

    def nr_flush():
        while nr_pending:
            nr_pending.pop(0)()

    def norm_rope(ps, rows, W, dim_eps, gcol, ones_ap, rmat, cosv, sinv, rope_lo, out_views, out_rows_list,
                  after=None):
        sq = Fr.next()
        kb.act(sq.ap[0:rows, 0:W], ps.ap[0:rows, 0:W], AF.Square, [ps], [sq])
        ssq = pss.next()
        kb.mm(ssq.ap[0:rows, 0:W], ones_ap, sq.ap[0:rows, 0:W], True, True, [sq, u_const], [ssq])

        def stage_b():
            r = rstd_from(ssq, rows, W, dim_eps)
            xn = Fr.next()
            kb.stt(xn.ap[0:rows, 0:W], ps.ap[0:rows, 0:W], gcol, r.ap[0:rows, 0:W], ALU.mult, ALU.mult,
                   [ps, r, u_small], [xn])
            xb = Br.next()
            kb.copy("dve", xb.ap[0:rows, 0:W], xn.ap[0:rows, 0:W], [xn], [xb])
            rot = ssq
            kb.mm(rot.ap[0:rows, 0:W], rmat, xb.ap[0:rows, 0:W], True, True, [xb, u_const], [rot])
            t1 = Fr.next()
            kb.tt("pool", t1.ap[rope_lo:rows, 0:W], xn.ap[rope_lo:rows, 0:W], cosv.ap[rope_lo:rows, 0:W], ALU.mult,
                  [xn, cosv], [t1])
            t2 = Fr.next()
            kb.tt("dve", t2.ap[rope_lo:rows, 0:W], rot.ap[rope_lo:rows, 0:W], sinv.ap[rope_lo:rows, 0:W], ALU.mult,
                  [rot, sinv], [t2])
            for (ov, r0, r1) in out_rows_list:
                if r0 < rope_lo:
                    e = min(r1, rope_lo)
                    kb.copy("act", ov.ap[r0:e, 0:W], xn.ap[r0:e, 0:W], [xn], [ov])
                if r1 > rope_lo:
                    s_ = max(r0, rope_lo)
                    kb.tt("dve", ov.ap[s_:r1, 0:W], t1.ap[s_:r1, 0:W], t2.ap[s_:r1, 0:W], ALU.add, [t1, t2], [ov])
            if after is not None:
                after()
        prev = nr_pending[:]
        del nr_pending[:]
        nr_pending.append(stage_b)
        for f in prev:
            f()

    for l in range(n_layers):
        last_layer = (l == 1)
        spo = l * SP_N
        kb.dma("sp", wuq_t[:, :, :], wb_uq[l].rearrange("(k p) n -> p k n", p=128), wunits("uq", l, 0, 2), [u_mlaw])
        kb.dma("sp", wukv_t[:, :], wb_ukv[l], wunits("ukv", l, 0, 1), [u_mlaw])
        kb.memset("pool", wkpad_t[:, :, :], 0.0, [u_mlaw])
        kb.copy("pool", wkpad_t[:, :, 0:64], wukv_t[:, :].rearrange("p (h c) -> p h c", c=128)[:, :, 0:64],
                [u_mlaw], [u_mlaw])
        kb.copy("pool", wv_t[:, :].rearrange("p (h c) -> p h c", c=64),
                wukv_t[:, :].rearrange("p (h c) -> p h c", c=128)[:, :, 64:128], [u_mlaw], [u_mlaw])

        for ti in range(n_tiles):
            b0, nb = TILES[ti]
            W = nb * 128
            s0 = b0 * 128
            nkb = b0 + nb
            tile_needs_out = not (last_layer and ti == 0)

            if l == 0:
                for b in range(nb):
                    xin = xio.next()
                    kb.dma("sp", xin.ap[:, :], xp[s0 + b * 128:s0 + (b + 1) * 128, :], [], [xin])
                    for half in range(2):
                        ps = pst.next()
                        for j in range(4):
                            kc = half * 4 + j
                            kb.tr(ps.ap[:, j * 128:(j + 1) * 128], xin.ap[:, kc * 128:(kc + 1) * 128], cfv(CF_IDENT),
                                  [xin, u_const], [ps])
                        for j in range(4):
                            kc = half * 4 + j
                            kb.copy("dve" if j % 2 == 0 else "act", hres[kc].ap[:, b * 128:(b + 1) * 128],
                                    ps.ap[:, j * 128:(j + 1) * 128], [ps], [hres[kc]])
            else:
                kb.dma("sp", hres_t[:, :, 0:W], hs[l - 1][:, :, s0:s0 + W], [ucache(u_hs, (l - 1, ti))], hres)
            kb.dma("sp", ropd[0].ap[:, 0:W], ropd_d[0][:, s0:s0 + W], [], [ropd[0]])
            kb.dma("sp", ropd[1].ap[:, 0:W], ropd_d[1][:, s0:s0 + W], [], [ropd[1]])
            kb.dma("sp", ropm[0].ap[0:96, 0:W], ropm_d[0][:, s0:s0 + W], [], [ropm[0]])
            kb.dma("sp", ropm[1].ap[0:96, 0:W], ropm_d[1][:, s0:s0 + W], [], [ropm[1]])

            rmsnorm_stream(W, l, SP_LN1)

            blk = load_wblk("in", wb_in, l, 0, 8, 0, 512)
            pcq = [psd.next(), psd.next()]
            for j in range(2):
                proj_fm(pcq[j], blk, j * 128, W, 8, hT)
            ssq = pss.next()
            for j in range(2):
                sq = Fr.next()
                kb.act(sq.ap[:, 0:W], pcq[j].ap[:, 0:W], AF.Square, [pcq[j]], [sq])
                kb.mm(ssq.ap[:, 0:W], cfv(CF_ONES), sq.ap[:, 0:W], j == 0, j == 1, [sq, u_const], [ssq])
            r = rstd_from(ssq, 128, W, 256)
            for j in range(2):
                kb.stt(cqn[j].ap[:, 0:W], pcq[j].ap[:, 0:W], smalls[:, spo + SP_CQ + j:spo + SP_CQ + j + 1],
                       r.ap[:, 0:W], ALU.mult, ALU.mult, [pcq[j], r, u_small], [cqn[j]])
            pkv = psd.next()
            proj_fm(pkv, blk, 256, W, 8, hT)
            sq = Fr.next()
            kb.act(sq.ap[:, 0:W], pkv.ap[:, 0:W], AF.Square, [pkv], [sq])
            ssq = pss.next()
            kb.mm(ssq.ap[:, 0:W], cfv(CF_ONES), sq.ap[:, 0:W], True, True, [sq, u_const], [ssq])
            r = rstd_from(ssq, 128, W, 128)
            kb.stt(ckvn.ap[:, 0:W], pkv.ap[:, 0:W], smalls[:, spo + SP_CKV:spo + SP_CKV + 1], r.ap[:, 0:W],
                   ALU.mult, ALU.mult, [pkv, r, u_small], [ckvn])
            pkr = psd.next()
            proj_fm(pkr, blk, 384, W, 8, hT)
            kb.copy("dve", kr.ap[0:32, 0:W], pkr.ap[0:32, 0:W], [pkr], [kr])

            for h in range(8):
                ps = psd.next()
                for j in range(2):
                    kb.mm(ps.ap[0:96, 0:W], wuq_t[:, j, h * 96:(h + 1) * 96], cqn[j].ap[:, 0:W], j == 0, j == 1,
                          [cqn[j], u_mlaw], [ps])
                norm_rope(ps, 96, W, 96, smalls[0:96, spo + SP_MQ:spo + SP_MQ + 1], cfm[0:96, CF_ONES, 0:96],
                          cmat[0:96, CM_RM, 0:96], ropm[0], ropm[1], 64, None, [(Qm[h], 0, 96)])
            for h in range(8):
                ps = psd.next()
                kb.mm(ps.ap[0:96, 0:W], wkpad_t[:, h, :], ckvn.ap[:, 0:W], True, False, [ckvn, u_mlaw], [ps])
                kb.mm(ps.ap[0:96, 0:W], cmat[:, CM_SEL, 0:96], kr.ap[:, 0:W], False, True, [kr, u_const], [ps])
                kst = Br.next()

                def _store_km(kst=kst, h=h):
                    kb.dma("pool", kc_m[l, h][:, s0:s0 + W], kst.ap[0:96, 0:W], [kst], [ucache(u_kc, ("m", l, h))])
                norm_rope(ps, 96, W, 96, smalls[0:96, spo + SP_MK:spo + SP_MK + 1], cfm[0:96, CF_ONES, 0:96],
                          cmat[0:96, CM_RM, 0:96], ropm[0], ropm[1], 64, None, [(kst, 0, 96)], after=_store_km)
            nr_flush()
            vst = Vst.next()
            for b in range(nb):
                ps = psd.next()
                kb.mm(ps.ap[:, :], ckvn.ap[:, b * 128:(b + 1) * 128], wv_t[:, :], True, True, [ckvn, u_mlaw], [ps])
                kb.copy("act", vst.ap[:, b, :].rearrange("p (h c) -> p h c", c=65)[:, :, 0:64],
                        ps.ap[:, :].rearrange("p (h c) -> p h c", c=64), [ps], [vst])
            if ti == 0:
                kb.memset("pool", vst.ap[0:PAD, 0, :], 0.0, [vst])
            kb.dma("pool", vc_m[l][:, b0:b0 + nb, :], vst.ap[:, 0:nb, :], [vst], [ucache(u_vc, ("m", l))])
            kb.memset("pool", vst.ap[:, :, :], 1.0, [vst])

            blk = load_wblk("in", wb_in, l, 0, 8, C_SBQ, 512)
            for pr in range(4):
                ps = psd.next()
                proj_fm(ps, blk, pr * 128, W, 8, hT)
                kb.ts("dve", Qs[2 * pr].ap[0:64, 0:W], ps.ap[0:64, 0:W], 0.125, None, ALU.mult, None, [ps], [Qs[2 * pr]])
                kb.act(Qs[2 * pr + 1].ap[64:128, 0:W], ps.ap[64:128, 0:W], AF.Copy, [ps], [Qs[2 * pr + 1]], scale=0.125)
            blk = load_wblk("in", wb_in, l, 0, 8, C_SBK, 512)
            for pr in range(4):
                ps = psd.next()
                proj_fm(ps, blk, pr * 128, W, 8, hT)
                kst = Br.next()
                kb.copy("dve" if pr % 2 else "act", kst.ap[:, 0:W], ps.ap[:, 0:W], [ps], [kst])
                kb.dma("pool", kc_s[l, pr][:, s0:s0 + W], kst.ap[:, 0:W], [kst], [ucache(u_kc, ("s", l, pr))])
            blk = load_wblk("in", wb_in, l, 0, 8, C_SBV, 512)
            vst = Vst.next()
            for b in range(nb):
                ps = psd.next()
                for kc in range(8):
                    kb.mm(ps.ap[:, :], hT[kc].ap[:, b * 128:(b + 1) * 128], blk.ap[:, kc, :], kc == 0, kc == 7,
                          [hT[kc], blk], [ps])
                kb.copy("act" if b % 2 else "dve", vst.ap[:, b, 0:512], ps.ap[:, :], [ps], [vst])
            if ti == 0:
                kb.memset("pool", vst.ap[0:PAD, 0, :], 0.0, [vst])
            kb.dma("pool", vc_s[l][:, b0:b0 + nb, :], vst.ap[:, 0:nb, 0:512], [vst], [ucache(u_vc, ("s", l))])
            kb.memset("pool", vst.ap[:, :, :], 1.0, [vst])

            for (c0, gcol, isq) in ((C_DFQ, SP_DQ, True), (C_DFK, SP_DK, False)):
                blk = load_wblk("in", wb_in, l, 0, 8, c0, 512)
                for h in range(4):
                    ps = psd.next()
                    proj_fm(ps, blk, h * 128, W, 8, hT)
                    if isq:
                        outs = [(Qd[2 * h], 0, 64), (Qd[2 * h + 1], 64, 128)]
                        kst = None
                    else:
                        kst = Br.next()
                        outs = [(kst, 0, 128)]
                    aft = None
                    if not isq:
                        def aft(kst=kst, h=h):
                            kb.dma("pool", kc_d[l, h][:, s0:s0 + W], kst.ap[:, 0:W], [kst], [ucache(u_kc, ("d", l, h))])
                    norm_rope(ps, 128, W, 64, smalls[:, spo + gcol:spo + gcol + 1], cfv(CF_BONES),
                              cmv(CM_RD), ropd[0], ropd[1], 0, None, outs, after=aft)
            nr_flush()
            blk = load_wblk("in", wb_in, l, 0, 8, C_DFV, 512)
            vst = Vst.next()
            for b in range(nb):
                ps = psd.next()
                for kc in range(8):
                    kb.mm(ps.ap[:, :], hT[kc].ap[:, b * 128:(b + 1) * 128], blk.ap[:, kc, :], kc == 0, kc == 7,
                          [hT[kc], blk], [ps])
                kb.copy("act" if b % 2 else "dve", vst.ap[:, b, 0:516].rearrange("p (h c) -> p h c", c=129)[:, :, 0:128],
                        ps.ap[:, :].rearrange("p (h c) -> p h c", c=128), [ps], [vst])
            if ti == 0:
                kb.memset("pool", vst.ap[0:PAD, 0, :], 0.0, [vst])
            kb.dma("pool", vc_d[l][:, b0:b0 + nb, :], vst.ap[:, 0:nb, 0:516], [vst], [ucache(u_vc, ("d", l))])
            kb.memset("pool", vst.ap[:, :, :], 1.0, [vst])

            if debug and l == 0 and ti == DBG_TI:
                for i, Q in enumerate((Qm, Qs, Qd)):
                    for j in range(8):
                        kb.dma("pool", dbg_q[i][:, j, 0:W], Q[j].ap[:, 0:W], [Q[j]], [])
            if l == 0 and pending_cast:
                lim = len(pending_cast) if ti == n_tiles - 1 else cast_per_tile
                for job in pending_cast[:lim]:
                    job()
                del pending_cast[:lim]
            if phases < 2 or not tile_needs_out:
                continue

            pS = Ring(PS[0:3])
            pO = Ring(PS[3:5])
            pX = Ring(PS[5:8])

            def transpose_branch(br):
                for c in range(4):
                    ps = pX.next()
                    for qb in range(nb):
                        kb.tr(ps.ap[:, qb * 128:(qb + 1) * 128], Ost[qb].ap[:, c * 128:(c + 1) * 128], cfv(CF_IDENT),
                              [Ost[qb], u_const], [ps])
                    kb.copy("act" if c % 2 else "dve", OT[br * 4 + c].ap[:, 0:W], ps.ap[:, 0:W], [ps], [OT[br * 4 + c]])

            def kcols(kbi):
                if kbi < b0:
                    return 0, False
                return (kbi - b0) * 128, True

            for hg in (range(4) if ATT_BR & 1 else []):
                vb = VBUF[hg % 2]
                kb.dma("sp", vb.ap[:, 0:nkb, 0:130], vc_m[l][:, 0:nkb, hg * 130:(hg + 1) * 130],
                       [ucache(u_vc, ("m", l))], [vb])
                for hh in range(2):
                    h = hg * 2 + hh
                    kbuf = KBUF[h % 2]
                    kb.dma("sp", kbuf.ap[0:96, 0:nkb * 128], kc_m[l, h][:, 0:nkb * 128], [ucache(u_kc, ("m", l, h))], [kbuf])
                    po = pO.next()
                    pendq = []
                    LA = 2
                    for kbi in range(nkb + LA):
                        cur = None
                        pend = pendq.pop(0) if len(pendq) == LA or (kbi >= nkb and pendq) else None
                        if kbi < nkb:
                            c0, diag = kcols(kbi)
                            ps = pS.next()
                            kb.mm(ps.ap[:, c0:W], kbuf.ap[:, kbi * 128:(kbi + 1) * 128], Qm[h].ap[:, c0:W], True, True,
                                  [kbuf, Qm[h]], [ps])
                            pt = Br.next()
                            kb.act(pt.ap[:, c0:W], ps.ap[:, c0:W], AF.Exp, [ps], [pt])
                            if diag:
                                kb.tt("dve", pt.ap[:, c0:c0 + 128], pt.ap[:, c0:c0 + 128], cmv(CM_MINCL), ALU.mult,
                                      [pt, u_const], [pt])
                            cur = (kbi, c0, pt)
                        if pend is not None:
                            pk, pc0, ppt = pend
                            for qb in range(pc0 // 128, nb):
                                kb.mm(po.ap[:, qb * 65:(qb + 1) * 65], ppt.ap[:, qb * 128:(qb + 1) * 128],
                                      vb.ap[:, pk, hh * 65:(hh + 1) * 65], pk == 0 and qb == pc0 // 128, pk == b0 + qb,
                                      [ppt, vb], [po], sig=(qb == nb - 1))
                        if cur is not None:
                            pendq.append(cur)
                    rc = sm.next()
                    pov = po.ap[:, 0:nb * 65].rearrange("p (q c) -> p q c", c=65)
                    kb.ts("dve", rc.ap[:, 0:nb], pov[:, :, 64], 1e-30, None, ALU.max, None, [po], [rc])
                    kb.recip(rc.ap[:, 0:nb], rc.ap[:, 0:nb], [rc], [rc])
                    for qb in range(nb):
                        kb.ts("dve", Ost[qb].ap[:, h * 64:(h + 1) * 64], po.ap[:, qb * 65:qb * 65 + 64], rc.ap[:, qb:qb + 1],
                              None, ALU.mult, None, [po, rc], [Ost[qb]])
            transpose_branch(0)

            pAb = [[PS[0], PS[1]], [PS[2], PS[5]]]
            pC = [PS[6], PS[7]]
            for hg in (range(4) if ATT_BR & 2 else []):
                vb = VBUF[hg % 2]
                kb.dma("sp", vb.ap[:, 0:nkb, 0:128], vc_s[l][:, 0:nkb, hg * 128:(hg + 1) * 128],
                       [ucache(u_vc, ("s", l))], [vb])
                pr = hg
                kbuf = KBUF[pr % 2]
                kb.dma("sp", kbuf.ap[:, 0:nkb * 128], kc_s[l, pr][:, 0:nkb * 128], [ucache(u_kc, ("s", l, pr))], [kbuf])
                po = pO.next()
                for s in range(2):
                    kb.memset("pool", carry32[s].ap[0:65, :], 0.0, [carry32[s]])
                    kb.memset("pool", carryb[s].ap[0:65, :], 0.0, [carryb[s]])
                kbs = list(range(nkb - 1, -1, -1))
                nit = len(kbs)
                st = {}

                def sb_s1(i):
                    kbi = kbs[i]
                    c0, diag = kcols(kbi)
                    sps = []
                    Es = []
                    for s in range(2):
                        pa = pAb[s][i % 2]
                        q = Qs[2 * pr + s]
                        kb.mm(pa.ap[:, c0:W], kbuf.ap[:, kbi * 128:(kbi + 1) * 128], q.ap[:, c0:W], True, True,
                              [kbuf, q], [pa])
                        E = Fr.next()
                        kb.act(E.ap[:, c0:W], pa.ap[:, c0:W], AF.Exp, [pa], [E])
                        Es.append(E)
                    for s in range(2):
                        E = Es[s]
                        sp = Br.next()
                        kb.act(sp.ap[:, c0:W], E.ap[:, c0:W], AF.Ln, [E], [sp], bias=1.0, scale=1.0)
                        if diag:
                            kb.tt("dve", sp.ap[:, c0:c0 + 128], sp.ap[:, c0:c0 + 128], cmv(CM_MSTRICT), ALU.mult,
                                  [sp, u_const], [sp])
                        if kbi == 0:
                            kb.memset("pool", sp.ap[0:PAD, c0:W], 0.0, [sp])
                        sps.append(sp)
                    st[i] = {"c0": c0, "diag": diag, "sps": sps, "kbi": kbi}

                def sb_s2(i):
                    c0, sps, kbi = st[i]["c0"], st[i]["sps"], st[i]["kbi"]
                    for s in range(2):
                        pa = pAb[s][i % 2]
                        kb.mm(pa.ap[:, c0:W], cmv(CM_NEGTRI), sps[s].ap[:, c0:W], False, False,
                              [sps[s], u_const], [pa])
                        if kbi > 0:
                            kb.mm(pC[s].ap[:, c0:W], cmv(CM_NEGCOL), sps[s].ap[:, c0:W], True, True,
                                  [sps[s], u_const], [pC[s]])
                    for s in range(2):
                        pa = pAb[s][i % 2]
                        kb.mm(pa.ap[:, c0:W], cmv(CM_CARRYL), carryb[s].ap[:, c0:W], False, True,
                              [carryb[s], u_const], [pa])

                def sb_s3(i):
                    c0, diag, kbi = st[i]["c0"], st[i]["diag"], st[i]["kbi"]
                    pts = []
                    for s in range(2):
                        pa = pAb[s][i % 2]
                        pt = Br.next()
                        kb.act(pt.ap[:, c0:W], pa.ap[:, c0:W], AF.Exp, [pa], [pt])
                        if diag:
                            kb.tt("dve", pt.ap[:, c0:c0 + 128], pt.ap[:, c0:c0 + 128], cmv(CM_MSTRICT), ALU.mult,
                                  [pt, u_const], [pt])
                        pts.append(pt)
                        if kbi > 0:
                            kb.tt("dve", carry32[s].ap[0:65, c0:W], carry32[s].ap[0:65, c0:W], pC[s].ap[0:65, c0:W],
                                  ALU.add, [carry32[s], pC[s]], [carry32[s]])
                            kb.copy("dve", carryb[s].ap[0:65, c0:W], carry32[s].ap[0:65, c0:W], [carry32[s]], [carryb[s]])
                            kb.tt("pool", carryb[s].ap[0:1, c0:W], carry32[s].ap[0:1, c0:W], carryb[s].ap[0:1, c0:W],
                                  ALU.subtract, [carry32[s], carryb[s]], [carryb[s]])
                    st[i]["pts"] = pts

                def sb_s4(i):
                    c0, kbi, pts = st[i]["c0"], st[i]["kbi"], st[i]["pts"]
                    for s in range(2):
                        for qb in range(c0 // 128, nb):
                            kb.mm(po.ap[:, s * 256 + qb * 64:s * 256 + (qb + 1) * 64], pts[s].ap[:, qb * 128:(qb + 1) * 128],
                                  vb.ap[:, kbi, s * 64:(s + 1) * 64], i == 0 and s == 0 and qb == c0 // 128,
                                  kbi == 0, [pts[s], vb], [po], sig=(qb == nb - 1))
                    del st[i]

                for j in range(nit + 2):
                    if j < nit:
                        sb_s1(j)
                    if 0 <= j - 1 < nit:
                        sb_s2(j - 1)
                        sb_s3(j - 1)
                    if 0 <= j - 2 < nit:
                        sb_s4(j - 2)
                for s in range(2):
                    h = pr * 2 + s
                    for qb in range(nb):
                        kb.copy("dve", Ost[qb].ap[:, h * 64:(h + 1) * 64],
                                po.ap[:, s * 256 + qb * 64:s * 256 + (qb + 1) * 64], [po], [Ost[qb]])
            transpose_branch(1)

            pS4 = Ring([PS[0], PS[1], PS[2], PS[5]])
            pOd = [PS[3], PS[4]]
            pOx = PS[6]

            def oreg(m, qb, lo, hi):
                if qb < 3:
                    return pOd[m], pOd[m].ap[:, qb * 129 + lo:qb * 129 + hi]
                return pOx, pOx.ap[:, m * 129 + lo:m * 129 + hi]
            for hg in (range(4) if ATT_BR & 4 else []):
                vb = VBUF[hg % 2]
                kb.dma("sp", vb.ap[:, 0:nkb, 0:129], vc_d[l][:, 0:nkb, hg * 129:(hg + 1) * 129],
                       [ucache(u_vc, ("d", l))], [vb])
                h = hg
                kbuf = KBUF[h % 2]
                kb.dma("sp", kbuf.ap[:, 0:nkb * 128], kc_d[l, h][:, 0:nkb * 128], [ucache(u_kc, ("d", l, h))], [kbuf])
                pend = None
                for kbi in range(nkb + 1):
                    cur = None
                    if kbi < nkb:
                        c0, diag = kcols(kbi)
                        pts = []
                        for m in range(2):
                            ps = pS4.next()
                            q = Qd[2 * h + m]
                            kb.mm(ps.ap[:, c0:W], kbuf.ap[:, kbi * 128:(kbi + 1) * 128], q.ap[:, c0:W], True, True,
                                  [kbuf, q], [ps])
                            pt = Br.next()
                            kb.act(pt.ap[:, c0:W], ps.ap[:, c0:W], AF.Exp, [ps], [pt])
                            if diag:
                                kb.tt("dve", pt.ap[:, c0:c0 + 128], pt.ap[:, c0:c0 + 128], cmv(CM_MINCL), ALU.mult,
                                      [pt, u_const], [pt])
                            pts.append(pt)
                        cur = (kbi, c0, pts)
                    if pend is not None:
                        pk, pc0, ppts = pend
                        for m in range(2):
                            for qb in range(pc0 // 128, nb):
                                bank, oap = oreg(m, qb, 0, 129)
                                first = (pk == 0 and qb == 0) if qb < 3 else (pk == 0 and m == 0)
                                kb.mm(oap, ppts[m].ap[:, qb * 128:(qb + 1) * 128], vb.ap[:, pk, 0:129], first, pk == b0 + qb,
                                      [ppts[m], vb], [bank], sig=(qb == nb - 1))
                    pend = cur
                rc = sm.next()
                for m in range(2):
                    nq = min(nb, 3)
                    kb.ts("dve", rc.ap[:, m * 4:m * 4 + nq],
                          pOd[m].ap[:, 0:nq * 129].rearrange("p (q c) -> p q c", c=129)[:, :, 128], 1e-30, None,
                          ALU.max, None, [pOd[m]], [rc])
                    if nb == 4:
                        kb.ts("dve", rc.ap[:, m * 4 + 3:m * 4 + 4], pOx.ap[:, m * 129 + 128:m * 129 + 129], 1e-30, None,
                              ALU.max, None, [pOx], [rc])
                    kb.recip(rc.ap[:, m * 4:m * 4 + nb], rc.ap[:, m * 4:m * 4 + nb], [rc], [rc])
                kb.ts("dve", rc.ap[:, 4:4 + nb], rc.ap[:, 4:4 + nb], lamt[:, l, 1:2], None, ALU.mult, None, [rc, u_small], [rc])
                for qb in range(nb):
                    o1 = Fr.next()
                    bk0, oa0 = oreg(0, qb, 0, 128)
                    bk1, oa1 = oreg(1, qb, 0, 128)
                    kb.ts("dve", o1.ap[:, 0:128], oa0, rc.ap[:, qb:qb + 1], None, ALU.mult, None, [bk0, rc], [o1])
                    kb.stt(o1.ap[:, 128:256], oa1, rc.ap[:, 4 + qb:5 + qb], o1.ap[:, 0:128], ALU.mult, ALU.add,
                           [bk1, rc, o1], [o1])
                    ss = sm.next()
                    kb.act(o1.ap[:, 256:384], o1.ap[:, 128:256], AF.Square, [o1], [o1, ss], accum_out=ss.ap[:, 0:1])
                    kb.act(ss.ap[:, 1:2], ss.ap[:, 0:1], AF.Ln, [ss], [ss], bias=float(EPS), scale=1.0 / 128.0)
                    kb.act(ss.ap[:, 2:3], ss.ap[:, 1:2], AF.Exp, [ss], [ss], scale=-0.5)
                    kb.stt(Ost[qb].ap[:, h * 128:(h + 1) * 128], o1.ap[:, 128:256], ss.ap[:, 2:3], gout[:, l, :],
                           ALU.mult, ALU.mult, [o1, ss, u_small], [Ost[qb]])
            transpose_branch(2)

            if debug and l == 0 and ti == DBG_TI:
                kb.dma("pool", dbg_o[:, :, 0:W], OT_t[:, :, 0:W], OT, [])
            if phases < 3:
                continue

            psg = Ring(PS[0:3])
            psy = Ring(PS[3:6])
            pso = Ring(PS[6:8])
            for half in range(2):
                macc = Ost
                for br in range(3):
                    gblk = load_wblk("in", wb_in, l, 0, 8, C_GATE + br * 1024 + half * 512, 512)
                    bblk = load_wblk("br", wb_br, l, br * 4, 4, half * 512, 512)
                    for j in range(4):
                        cb = half * 4 + j
                        pg = psg.next()
                        proj_fm(pg, gblk, j * 128, W, 8, hT)
                        py = psy.next()
                        proj_fm(py, bblk, j * 128, W, 4, OT[br * 4:br * 4 + 4])
                        g = Fr.next()
                        gc = spo + SP_GB + br * 8 + cb
                        kb.act(g.ap[:, 0:W], pg.ap[:, 0:W], AF.Sigmoid, [pg, u_small], [g], bias=smallp[:, gc:gc + 1], scale=1.0)
                        if br == 0:
                            kb.tt("dve", macc[j].ap[:, 0:W], py.ap[:, 0:W], g.ap[:, 0:W], ALU.mult, [py, g], [macc[j]])
                        else:
                            kb.tt("dve", g.ap[:, 0:W], py.ap[:, 0:W], g.ap[:, 0:W], ALU.mult, [py, g], [g])
                            if br == 1:
                                kb.tt("pool", macc[j].ap[:, 0:W], macc[j].ap[:, 0:W], g.ap[:, 0:W], ALU.add, [macc[j], g], [macc[j]])
                            else:
                                kb.tt("pool", mergedT[cb].ap[:, 0:W], macc[j].ap[:, 0:W], g.ap[:, 0:W], ALU.add,
                                      [macc[j], g], [mergedT[cb]])
            for half in range(2):
                oblk = load_wblk("out", wb_out, l, 0, 8, half * 512, 512)
                for j in range(4):
                    cb = half * 4 + j
                    ps = pso.next()
                    proj_fm(ps, oblk, j * 128, W, 8, mergedT)
                    kb.tt("dve", hres[cb].ap[:, 0:W], hres[cb].ap[:, 0:W], ps.ap[:, 0:W], ALU.add, [hres[cb], ps], [hres[cb]])
            if phases < 4:
                kb.dma("pool", hs[l][:, :, s0:s0 + W], hres_t[:, :, 0:W], hres, [ucache(u_hs, (l, ti))])
                continue

            rmsnorm_stream(W, l, SP_LN2)
            psf = Ring(PS[0:4])
            for fb in range(8):
                blk = load_wblk("ff1", wb_ff1, l, 0, 8, fb * 512, 512)
                for j in range(4):
                    ps = psf.next()
                    proj_fm(ps, blk, j * 128, W, 8, hT)
                    rl = Fr.next()
                    kb.act(rl.ap[:, 0:W], ps.ap[:, 0:W], AF.Relu, [ps], [rl])
                    a = actT[fb * 4 + j]
                    kb.tt("pool" if j % 2 else "dve", a.ap[:, 0:W], rl.ap[:, 0:W], rl.ap[:, 0:W], ALU.mult, [rl], [a])
            for half in range(2):
                banks = PS[4:8]
                for g4 in range(4):
                    blk = load_wblk("ff2", wb_ff2, l, g4 * 8, 8, half * 512, 512)
                    for fc in range(8):
                        for j in range(4):
                            kb.mm(banks[j].ap[:, 0:W], blk.ap[:, fc, j * 128:(j + 1) * 128], actT[g4 * 8 + fc].ap[:, 0:W],
                                  g4 == 0 and fc == 0, g4 == 3 and fc == 7, [blk, actT[g4 * 8 + fc]], [banks[j]],
                                  sig=(j == 3))
                for j in range(4):
                    cb = half * 4 + j
                    kb.tt("dve", hres[cb].ap[:, 0:W], hres[cb].ap[:, 0:W], banks[j].ap[:, 0:W], ALU.add,
                          [hres[cb], banks[j]], [hres[cb]])
            for v in X[0:30]:
                kb.memset("pool", v.ap[:, :], 0.0, [v])

            if not last_layer or debug:
                kb.dma("pool", hs[l][:, :, s0:s0 + W], hres_t[:, :, 0:W], hres, [ucache(u_hs, (l, ti))])
            if last_layer:
                for b in range(nb):
                    xo = xio.next()
                    for half in range(2):
                        ps = PS[half]
                        for j in range(4):
                            kc = half * 4 + j
                            kb.tr(ps.ap[:, j * 128:(j + 1) * 128], hres[kc].ap[:, b * 128:(b + 1) * 128], cfv(CF_IDENT),
                                  [hres[kc], u_const], [ps])
                        kb.copy("act" if half else "dve", xo.ap[:, half * 512:(half + 1) * 512], ps.ap[:, :], [ps], [xo])
                    r0 = s0 + b * 128 - 128
                    kb.dma("sp", out_d[r0:r0 + 128, :], xo.ap[:, :], [xo], [u_out])

    kb.final_wait("sp", [u_out] + list(u_hs.values()) + list(u_kc.values()) + list(u_vc.values()))
    with nc.Block() as block:
        kb.emit(block)
    es.close()
    return nc, kb


def _host_inputs(x, meta_tokens, ln1_g, w_in, mla_cq_norm_g, mla_ckv_norm_g, mla_w_uq, mla_w_ukv,
                 mla_q_norm_g, mla_k_norm_g, diff_q_norm_g, diff_k_norm_g, diff_lambda,
                 diff_out_norm_g, gate_b, w_branch, w_out, ln2_g, w_ff1, w_ff2):
    f = np.float32
    smallp = np.zeros((128, 2 * SP_N), f)
    bcastp = np.zeros((128, 2 * 384), f)
    for l in range(2):
        o = l * SP_N
        smallp[:, o + SP_LN1:o + SP_LN1 + 8] = np.asarray(ln1_g[l], f).reshape(8, 128).T
        smallp[:, o + SP_LN2:o + SP_LN2 + 8] = np.asarray(ln2_g[l], f).reshape(8, 128).T
        smallp[:, o + SP_CQ:o + SP_CQ + 2] = np.asarray(mla_cq_norm_g[l], f).reshape(2, 128).T
        smallp[:, o + SP_CKV] = np.asarray(mla_ckv_norm_g[l], f)
        smallp[:96, o + SP_MQ] = np.asarray(mla_q_norm_g[l], f)
        smallp[:96, o + SP_MK] = np.asarray(mla_k_norm_g[l], f)
        smallp[:, o + SP_DQ] = np.tile(np.asarray(diff_q_norm_g[l], f), 2)
        smallp[:, o + SP_DK] = np.tile(np.asarray(diff_k_norm_g[l], f), 2)
        smallp[:, o + SP_GB:o + SP_GB + 24] = np.asarray(gate_b[l], f).reshape(24, 128).T
        bo = l * 384
        bcastp[:, bo:bo + 128] = np.asarray(diff_out_norm_g[l], f)[None, :]
        bcastp[:, bo + 128:bo + 384] = np.asarray(diff_lambda[l], f).reshape(1, 256)
    cm, cf, ropd, ropm = _consts()
    common = dict(w_in=np.ascontiguousarray(w_in, f), mla_w_uq=np.ascontiguousarray(mla_w_uq, f),
                  mla_w_ukv=np.ascontiguousarray(mla_w_ukv, f), w_branch=np.ascontiguousarray(w_branch, f),
                  w_out=np.ascontiguousarray(w_out, f), w_ff1=np.ascontiguousarray(w_ff1, f),
                  w_ff2=np.ascontiguousarray(w_ff2, f), smallp=smallp, bcastp=bcastp, cmat=cm, cfmat=cf,
                  ropd=ropd, ropm=ropm)
    maps = []
    x = np.asarray(x, f)
    meta = np.asarray(meta_tokens, f)
    for c in range(8):
        b = c % 4
        xp = np.concatenate([np.zeros((PAD, D), f), meta, x[b]], 0)
        m = dict(common)
        m["xp"] = np.ascontiguousarray(xp)
        maps.append(m)
    return maps


_PROG = None
import os as _os
DBG_TI = int(_os.environ.get('DBG_TI', '1'))
ATT_BR = int(_os.environ.get('ATT_BR', '7'))


def kernel(**inputs):
    global _PROG
    maps = _host_inputs(**inputs)
    if _PROG is None:
        _PROG = build_program()[0]
    res = run_bass_kernel_spmd(_PROG, maps, core_ids=list(range(8)))
    out = np.stack([np.asarray(res.results[c]["out"], np.float32) for c in range(4)], 0)
    return out
```
